# Optimizing a Trainium2 kernel written in Bass

```python
import jax, jax.numpy as jnp
from jax import lax
import numpy as np

D_MODEL = 2048
BATCH = 4
SEQ = 4096
DEPTH = 4

N_MIXERS = 2
N_POOL_LAYERS = (DEPTH + 1) // 2
N_FOX_LAYERS = DEPTH // 2

POOL_WINDOWS = (2, 4, 8, 16)
N_POOL_GROUPS = len(POOL_WINDOWS)
POOL_GROUP = D_MODEL // N_POOL_GROUPS

HEAD_DIM = 64
N_HEADS = D_MODEL // HEAD_DIM
Q_BLOCK = 128
ATTN_SCALE = HEAD_DIM ** -0.5
FOX_IN = 4 * D_MODEL + N_HEADS

FFN_HIDDEN = ((8 * D_MODEL // 3 + 255) // 256) * 256

RMS_EPS = 1e-6

kernel_name = "hybrid_pool_fox_swiglu_trunk"


def rmsnorm(x, g):
    xf = x.astype(jnp.float32)
    y = xf * lax.rsqrt(jnp.mean(xf * xf, axis=-1, keepdims=True) + RMS_EPS)
    return (y * g.astype(jnp.float32)).astype(x.dtype)


def pool_mixer(h, w_group, scale):
    B, S, D = h.shape
    hf = h.astype(jnp.float32)
    cs = jnp.cumsum(hf, axis=1)
    n_valid = jnp.arange(1, S + 1, dtype=jnp.float32)[:, None]
    feats = []
    for g, win in enumerate(POOL_WINDOWS):
        sl = slice(g * POOL_GROUP, (g + 1) * POOL_GROUP)
        cg = cs[..., sl]
        prev = jnp.pad(cg, ((0, 0), (win, 0), (0, 0)))[:, :S]
        mean = (cg - prev) / jnp.minimum(n_valid, float(win))
        feats.append(mean - hf[..., sl])
    f = jnp.stack(feats, axis=2).astype(h.dtype)
    y = jnp.einsum('bsgc,gcd->bsgd', f, w_group).reshape(B, S, D)
    return y * scale


def fox_mixer(h, w_in, b_f, q_norm_g, k_norm_g, w_out):
    B, S, D = h.shape
    proj = h @ w_in
    q = rmsnorm(proj[..., 0 * D:1 * D].reshape(B, S, N_HEADS, HEAD_DIM), q_norm_g)
    k = rmsnorm(proj[..., 1 * D:2 * D].reshape(B, S, N_HEADS, HEAD_DIM), k_norm_g)
    v = proj[..., 2 * D:3 * D].reshape(B, S, N_HEADS, HEAD_DIM)
    og = proj[..., 3 * D:4 * D]
    log_f = jax.nn.log_sigmoid(proj[..., 4 * D:].astype(jnp.float32) + b_f.astype(jnp.float32))
    c = jnp.transpose(jnp.cumsum(log_f, axis=1), (0, 2, 1))
    outs = []
    for i in range(S // Q_BLOCK):
        qs, qe = i * Q_BLOCK, (i + 1) * Q_BLOCK
        s = jnp.einsum('bqhd,bkhd->bhqk', q[:, qs:qe], k[:, :qe],
                       preferred_element_type=jnp.float32) * ATTN_SCALE
        s = s + c[:, :, qs:qe, None] - c[:, :, None, :qe]
        causal = jnp.arange(qs, qe)[:, None] >= jnp.arange(qe)[None, :]
        s = jnp.where(causal, s, -jnp.inf)
        p = jax.nn.softmax(s, axis=-1).astype(v.dtype)
        outs.append(jnp.einsum('bhqk,bkhd->bqhd', p, v[:, :qe]))
    o = jnp.concatenate(outs, axis=1).reshape(B, S, D)
    o = o * jax.nn.sigmoid(og)
    return o @ w_out


def swiglu(h, w_gate_up, w_down):
    gu = h @ w_gate_up
    gate, up = gu[..., :FFN_HIDDEN], gu[..., FFN_HIDDEN:]
    return (jax.nn.silu(gate) * up) @ w_down


def setup_inputs(seed: int = 0) -> dict:
    key = jax.random.key(seed)
    ks = jax.random.split(key, 16)
    f32 = jnp.float32
    D = D_MODEL

    def nrm(k, shape, fan_in):
        return jax.random.normal(k, shape, f32) * (fan_in ** -0.5)

    def gain(k, shape):
        return 1.0 + 0.05 * jax.random.normal(k, shape, f32)

    return {
        "x": jax.random.normal(ks[0], (BATCH, SEQ, D), f32),
        "attn_norm_g": gain(ks[1], (DEPTH, D)),
        "ffn_norm_g": gain(ks[2], (DEPTH, D)),
        "final_norm_g": gain(ks[3], (D,)),
        "pool_w": nrm(ks[4], (N_POOL_LAYERS, N_POOL_GROUPS, POOL_GROUP, POOL_GROUP), POOL_GROUP),
        "pool_scale": gain(ks[5], (N_POOL_LAYERS, D)),
        "fox_w_in": nrm(ks[6], (N_FOX_LAYERS, D, FOX_IN), D),
        "fox_b_f": jax.random.uniform(ks[7], (N_FOX_LAYERS, N_HEADS), f32, minval=1.0, maxval=5.0),
        "fox_q_norm_g": gain(ks[8], (N_FOX_LAYERS, HEAD_DIM)),
        "fox_k_norm_g": gain(ks[9], (N_FOX_LAYERS, HEAD_DIM)),
        "fox_w_out": nrm(ks[10], (N_FOX_LAYERS, D, D), D),
        "ffn_w_gate_up": nrm(ks[11], (DEPTH, D, 2 * FFN_HIDDEN), D),
        "ffn_w_down": nrm(ks[12], (DEPTH, FFN_HIDDEN, D), FFN_HIDDEN),
    }


def reference(x, attn_norm_g, ffn_norm_g, final_norm_g, pool_w, pool_scale,
              fox_w_in, fox_b_f, fox_q_norm_g, fox_k_norm_g, fox_w_out,
              ffn_w_gate_up, ffn_w_down):
    h = x
    for i in range(DEPTH):
        hn = rmsnorm(h, attn_norm_g[i])
        j = i // N_MIXERS
        if i % N_MIXERS == 0:
            mix = pool_mixer(hn, pool_w[j], pool_scale[j])
        else:
            mix = fox_mixer(hn, fox_w_in[j], fox_b_f[j], fox_q_norm_g[j],
                            fox_k_norm_g[j], fox_w_out[j])
        h = h + mix
        h = h + swiglu(rmsnorm(h, ffn_norm_g[i]), ffn_w_gate_up[i], ffn_w_down[i])
    return rmsnorm(h, final_norm_g)
```

```python
from contextlib import ExitStack

import numpy as np
import concourse.bass as bass
import concourse.mybir as mybir
from concourse.bass_utils import run_bass_kernel_spmd

F32 = mybir.dt.float32
BF16 = mybir.dt.bfloat16
AF = mybir.ActivationFunctionType
ALU = mybir.AluOpType
AX = mybir.AxisListType

D = 2048
NCH = 16
TT = 512
FF = 5632
NFC = 44
NH = 32
HD = 64
FOX_IN = 4 * D + NH
EPS = 1e-6
WINS = (2, 4, 8, 16)
SLABW = 256
SLOT_COLS = 22 * SLABW
NSLOT = 3
MASKV = -30000.0
SEM_MAX = 4000

C_ID = 0
C_NTRI = 128
C_MASK = 256
C_SEL = 384
C_G = 512
C_BF = C_G + 176
C_QG = C_BF + 64
C_KG = C_QG + 128
CW = C_KG + 128


class Tok:
    __slots__ = ("eng", "sem", "val")

    def __init__(self, eng, sem, val):
        self.eng, self.sem, self.val = eng, sem, val


class Buf:
    def __init__(self, name, const=False):
        self.name = name
        self.w = None
        self.r = []
        self.const = const


class Prog:
    ENGS = ("pe", "act", "dve", "pool", "sp")

    def __init__(self, nc, stack):
        self.nc = nc
        self.stack = stack
        self.lists = {e: [] for e in self.ENGS}
        self.cur = {}
        self.lanes = {}
        self.waited = {e: {} for e in self.ENGS}
        self.nsem = 0
        self.dry = False

    def newsem(self):
        s = self.stack.enter_context(self.nc.semaphore(f"sm{self.nsem}"))
        self.nsem += 1
        return s

    def _tok(self, eng):
        c = self.cur.get(eng)
        if c is None or c[1] >= SEM_MAX:
            c = [self.newsem(), 0]
            self.cur[eng] = c
        c[1] += 1
        return Tok(eng, c[0], c[1])

    def _lane(self, eng, lane):
        c = self.lanes.get(lane)
        if c is None or c[1] >= SEM_MAX:
            c = [self.newsem(), 0]
            self.lanes[lane] = c
        c[1] += 16
        return Tok("dma", c[0], c[1])

    def _waits(self, eng, reads, writes, deps):
        d = [t for t in deps if t is not None]
        for b in reads:
            if b.w is not None:
                d.append(b.w)
        for b in writes:
            if b.w is not None:
                d.append(b.w)
            d.extend(t for t in b.r if t is not None)
        best = {}
        for t in d:
            if eng == "pe" and t.eng == "pe":
                continue
            k = id(t.sem)
            if self.waited[eng].get(k, -1) >= t.val:
                continue
            if k not in best or best[k][1] < t.val:
                best[k] = (t.sem, t.val)
        for k, (s, v) in best.items():
            self.waited[eng][k] = v
        return list(best.values())

    def _note(self, tok, reads, writes):
        for b in reads:
            if not b.const:
                b.r.append(tok)
        for b in writes:
            b.w = tok
            b.r = []

    def op(self, eng, fn, reads=(), writes=(), deps=(), lane=None):
        if self.dry:
            return None
        waits = self._waits(eng, reads, writes, deps)
        if lane is not None:
            tok, inc = self._lane(eng, lane), 16
        else:
            tok, inc = self._tok(eng), 1
        self.lists[eng].append((waits, fn, tok, inc))
        self._note(tok, reads, writes)
        return tok

    def pe_group(self, fns, reads=(), writes=(), deps=()):
        if self.dry:
            return None
        waits = self._waits("pe", reads, writes, deps)
        tok = self._tok("pe")
        n = len(fns)
        for i, fn in enumerate(fns):
            self.lists["pe"].append((waits if i == 0 else [], fn, tok if i == n - 1 else None, 1))
        self._note(tok, reads, writes)
        return tok

    def run(self, eng, e):
        for waits, fn, tok, inc in self.lists[eng]:
            for s, v in waits:
                e.wait_ge(s, v)
            ins = fn(e)
            if tok is not None:
                ins.then_inc(tok.sem, inc)


def fence_of(bufs):
    toks = []
    for b in bufs:
        if b.w is not None:
            toks.append(b.w)
        toks.extend(t for t in b.r if t is not None)
    best = {}
    for t in toks:
        k = id(t.sem)
        if k not in best or best[k].val < t.val:
            best[k] = t
    return list(best.values())


class Region:
    def __init__(self):
        self.live = []

    def switch(self, names):
        f = fence_of(self.live)
        bufs = []
        for n in names:
            b = Buf(n)
            b.r = list(f)
            bufs.append(b)
        self.live = bufs
        return bufs


class WStream:
    def __init__(self, P, slots):
        self.P = P
        self.slots = slots
        self.reqs = []
        self.pos = 0
        self.issued = 0

    def _view(self, slot_t, nk, ncols):
        return slot_t[:, 0:nk * ncols].rearrange("p (k c) -> p k c", c=ncols)

    def get(self, src2d, r0, nk, c0, ncols, conv):
        i = self.pos
        self.pos += 1
        slot_t, slot_b = self.slots[i % len(self.slots)]
        view = self._view(slot_t, nk, ncols)
        if self.P.dry:
            self.reqs.append((src2d, r0, nk, c0, ncols, conv))
            return view, slot_b
        while self.issued < min(len(self.reqs), i + len(self.slots)):
            self._issue(self.issued)
            self.issued += 1
        return view, slot_b

    def _issue(self, i):
        src2d, r0, nk, c0, ncols, conv = self.reqs[i]
        slot_t, slot_b = self.slots[i % len(self.slots)]
        dst = self._view(slot_t, nk, ncols)
        src = src2d[r0:r0 + nk * 128, c0:c0 + ncols].rearrange("(k p) c -> p k c", p=128)
        self.P.op("sp", lambda e, dst=dst, src=src: e.dma_start(out=dst, in_=src),
                  writes=[slot_b], deps=[conv], lane=f"w{i % len(self.slots)}")


class KVStream:
    def __init__(self, P, name, slots):
        self.P = P
        self.name = name
        self.slots = slots
        self.n = 0
        self.reqs = []
        self.base = 0

    def begin(self, reqs):
        self.base = self.n
        self.reqs = reqs
        self.n += len(reqs)

    def issue(self, k):
        if k >= len(self.reqs):
            return
        dst_fn, src, deps = self.reqs[k]
        slot_t, slot_b = self.slots[(self.base + k) % len(self.slots)]
        dst = dst_fn(slot_t)
        self.P.op("sp", lambda e, dst=dst, src=src: e.dma_start(out=dst, in_=src),
                  writes=[slot_b], deps=deps, lane=f"{self.name}{(self.base + k) % len(self.slots)}")

    def slot(self, k):
        return self.slots[(self.base + k) % len(self.slots)]


def build(NT, layers, do_final):
    nc = bass.Bass("TRN2", target_bir_lowering=False)
    S = NT * TT
    x = nc.dram_tensor("x", [S, D], F32, kind="ExternalInput").ap()
    out = nc.dram_tensor("out", [S, D], F32, kind="ExternalOutput").ap()
    cst = nc.dram_tensor("cst", [128, CW], F32, kind="ExternalInput").ap()

    wsrc = {}

    def decl_w(name, rows, cols):
        a = nc.dram_tensor(name, [rows, cols], F32, kind="ExternalInput").ap()
        b = nc.dram_tensor(name + "_bf", [rows, cols], BF16, kind="Internal").ap()
        wsrc[name] = (a, b, rows, cols)

    worder = []
    for L in layers:
        j = L // 2
        if L % 2 == 0:
            decl_w(f"pw{j}", 2048, 512)
            worder.append(f"pw{j}")
        else:
            decl_w(f"win{j}", D, FOX_IN)
            decl_w(f"wout{j}", D, D)
            worder += [f"win{j}", f"wout{j}"]
        decl_w(f"wgu{L}", D, 2 * FF)
        decl_w(f"wdn{L}", FF, D)
        worder += [f"wgu{L}", f"wdn{L}"]
    fox_layers = [L for L in layers if L % 2 == 1]
    kc = {}
    vc = {}
    for L in fox_layers:
        kc[L] = nc.dram_tensor(f"kc{L}", [NT, 8, 72, 4 * TT], BF16, kind="Internal").ap()
        vc[L] = nc.dram_tensor(f"vc{L}", [NT, 8, 128, 4 * 2 * 192], BF16, kind="Internal").ap()

    with ExitStack() as st:
        def sb(name, shape, dt):
            return st.enter_context(nc.sbuf_tensor(name, shape, dt))

        P = Prog(nc, st)
        cst_sb = sb("cst_sb", [128, CW], F32)
        hT = sb("hT", [128, NCH, TT], F32)
        hn = sb("hn", [128, NCH, TT], BF16)
        R1 = sb("R1", [128, NFC * TT], BF16)
        OG = sb("OG", [128, NCH, TT], BF16)
        KS = [sb(f"KS{i}", [128, 4 * TT], BF16) for i in range(2)]
        KLs = [sb(f"KL{i}", [128, 4 * TT], BF16) for i in range(2)]
        VS = [sb(f"VS{i}", [128, 4 * 2 * 192], BF16) for i in range(2)]
        VLs = [sb(f"VL{i}", [128, 4 * 2 * 192], BF16) for i in range(2)]
        slots = [sb(f"slab{i}", [128, SLOT_COLS], BF16) for i in range(NSLOT)]
        PT = [sb(f"PT{i}", [128, TT], BF16) for i in range(3)]
        tmpf = [sb(f"tmpf{i}", [128, TT], F32) for i in range(2)]
        rstd = sb("rstd", [128, TT], F32)
        RR = sb("RR", [128, TT], F32)
        RC = sb("RC", [128, TT], F32)
        ident_bf = sb("ident_bf", [128, 128], BF16)
        ones_bf = sb("ones_bf", [128, 128], BF16)
        mask_bf = sb("mask_bf", [128, 128], BF16)
        negones = sb("negones", [128, 128], F32)
        halo = {L: sb(f"halo{L}", [128, NCH, 16], F32) for L in layers if L % 2 == 0}
        augq = sb("augq", [128, 4, 4, 72], BF16)
        augk = sb("augk", [128, 4, 4, 72], BF16)
        small = sb("small", [128, 1536], F32)
        smallb = sb("smallb", [128, 768], BF16)
        TOT = {L: sb(f"TOT{L}", [128, NT, NH], F32) for L in fox_layers}
        DTAB = sb("DTAB", [128, NT, NH], F32)
        qg_s = sb("qg_s", [128, 128], F32)
        psum = [st.enter_context(nc.psum_tensor(f"ps{i}", [128, TT], F32)) for i in range(8)]

        b_cst = Buf("cst", const=True)
        b_hT = [Buf(f"hT{c}") for c in range(NCH)]
        b_hn = [Buf(f"hn{c}") for c in range(NCH)]
        b_OG = [Buf(f"OG{c}") for c in range(NCH)]
        b_ps = [Buf(f"ps{i}") for i in range(8)]
        b_KS = [Buf(f"KS{i}") for i in range(2)]
        b_KL = [Buf(f"KL{i}") for i in range(2)]
        b_VS = [Buf(f"VS{i}") for i in range(2)]
        b_VL = [Buf(f"VL{i}") for i in range(2)]
        b_slots = [Buf(f"slab{i}") for i in range(NSLOT)]
        b_PT = [Buf(f"PT{i}") for i in range(3)]
        b_tmpf = [Buf(f"tmpf{i}") for i in range(2)]
        b_rstd = Buf("rstd")
        b_RR = Buf("RR")
        b_RC = Buf("RC")
        b_c2 = Buf("consts2", const=True)
        b_halo = {L: Buf(f"halo{L}") for L in halo}
        b_augq = Buf("augq")
        b_augk = Buf("augk")
        b_small = Buf("small")
        b_TOT = {L: Buf(f"TOT{L}") for L in fox_layers}
        b_DT = Buf("DTAB")
        reg1 = Region()

        W = WStream(P, list(zip(slots, b_slots)))
        KSt = KVStream(P, "kl", list(zip(KLs, b_KL)))
        VSt = KVStream(P, "vl", list(zip(VLs, b_VL)))

        rr = {"bank": 0, "tmp": 0, "alt": 0, "ks": 0, "vs": 0, "pt": 0}

        def nbank():
            i = rr["bank"]
            rr["bank"] = (i + 1) % 8
            return i

        def ntmp():
            i = rr["tmp"]
            rr["tmp"] = (i + 1) % 2
            return i

        def alt():
            rr["alt"] ^= 1
            return "act" if rr["alt"] else "dve"

        G = lambda col: cst_sb[:, C_G + col:C_G + col + 1]

        conv_tok = {}

        def emit_init():
            P.op("sp", lambda e: e.dma_start(out=cst_sb[:], in_=cst), writes=[b_cst], lane="cst")
            for name in worder:
                a, b, rows, cols = wsrc[name]
                n = 8
                rs = rows // n
                tok = None
                for i in range(n):
                    tok = P.op("pool", lambda e, a=a, b=b, i=i, rs=rs: e.dma_start(
                        out=b[i * rs:(i + 1) * rs, :], in_=a[i * rs:(i + 1) * rs, :]), lane="cv_" + name)
                conv_tok[name] = tok
            P.op("dve", lambda e: e.tensor_copy(out=ident_bf[:], in_=cst_sb[:, C_ID:C_ID + 128]),
                 reads=[b_cst], writes=[b_c2])
            P.op("dve", lambda e: e.tensor_copy(out=mask_bf[:], in_=cst_sb[:, C_MASK:C_MASK + 128]),
                 reads=[b_cst], writes=[b_c2])
            P.op("dve", lambda e: e.memset(ones_bf[:], 1.0), writes=[b_c2])
            P.op("dve", lambda e: e.memset(negones[:], -1.0), writes=[b_c2])
            P.op("dve", lambda e: e.memset(RR[:], 0.0), writes=[b_RR])
            P.op("dve", lambda e: e.tensor_scalar(out=qg_s[:], in0=cst_sb[:, C_QG:C_QG + 128],
                                                  scalar1=float(HD ** -0.5), scalar2=None, op0=ALU.mult),
                 reads=[b_cst], writes=[b_c2])
            for L in halo:
                P.op("dve", lambda e, L=L: e.memset(halo[L][:], 0.0), writes=[b_halo[L]])
            for t_, bb in list(zip(VS, b_VS)) + list(zip(VLs, b_VL)):
                v = t_[:].rearrange("p (a b c) -> p a b c", a=4, b=2)
                P.op("dve", lambda e, v=v: e.memset(v[:, :, :, 64:128], 0.0), writes=[bb])
                P.op("dve", lambda e, v=v: e.memset(v[:, :, :, 64:65], 1.0), writes=[bb])
            P.op("dve", lambda e: e.memset(augq[:], 0.0), writes=[b_augq])
            P.op("dve", lambda e: e.memset(augq[:, :, :, 67:70], 1.0), writes=[b_augq])
            P.op("dve", lambda e: e.memset(augk[:], 0.0), writes=[b_augk])
            P.op("dve", lambda e: e.memset(augk[:, :, :, 64:67], 1.0), writes=[b_augk])

        def load_x(T):
            (b_xs,) = reg1.switch(["xstage"])
            xs = R1[:, 0:16384].bitcast(F32).rearrange("p (t d) -> p t d", t=4)
            src = x[T * TT:(T + 1) * TT, :].rearrange("(t p) d -> p t d", p=128)
            P.op("sp", lambda e: e.dma_start(out=xs, in_=src), writes=[b_xs], lane="x")
            for c in range(NCH):
                bk = nbank()
                fns = [lambda e, tb=tb, c=c, bk=bk: e.transpose(
                    out=psum[bk][:, tb * 128:(tb + 1) * 128], in_=xs[:, tb, c * 128:(c + 1) * 128],
                    identity=cst_sb[:, C_ID:C_ID + 128]) for tb in range(4)]
                P.pe_group(fns, reads=[b_xs, b_cst], writes=[b_ps[bk]])
                eng = alt()
                if eng == "act":
                    P.op("act", lambda e, c=c, bk=bk: e.copy(out=hT[:, c, :], in_=psum[bk][:]),
                         reads=[b_ps[bk]], writes=[b_hT[c]])
                else:
                    P.op("dve", lambda e, c=c, bk=bk: e.tensor_copy(out=hT[:, c, :], in_=psum[bk][:]),
                         reads=[b_ps[bk]], writes=[b_hT[c]])

        def norm_stats():
            for c in range(NCH):
                P.op("act", lambda e, c=c: e.activation(out=hn[:, c, :], in_=hT[:, c, :], func=AF.Square),
                     reads=[b_hT[c]], writes=[b_hn[c]])
            bk = nbank()
            fns = [lambda e, c=c, bk=bk: e.matmul(psum[bk][:], lhsT=ones_bf[:], rhs=hn[:, c, :],
                                                  start=(c == 0), stop=(c == NCH - 1)) for c in range(NCH)]
            P.pe_group(fns, reads=b_hn + [b_c2], writes=[b_ps[bk]])
            P.op("act", lambda e, bk=bk: e.activation(out=rstd[:], in_=psum[bk][:], func=AF.Sqrt,
                                                      bias=EPS, scale=1.0 / D),
                 reads=[b_ps[bk]], writes=[b_rstd])
            P.op("dve", lambda e: e.reciprocal(out=rstd[:], in_=rstd[:]), reads=[b_rstd], writes=[b_rstd])

        def normalize(gcol0):
            for c in range(NCH):
                P.op("dve", lambda e, c=c: e.scalar_tensor_tensor(
                    out=hn[:, c, :], in0=hT[:, c, :], scalar=G(gcol0 + c), in1=rstd[:],
                    op0=ALU.mult, op1=ALU.mult),
                    reads=[b_hT[c], b_rstd, b_cst], writes=[b_hn[c]])

        def resid_add(dc, bk, scale_col=None):
            if scale_col is None:
                P.op("dve", lambda e: e.tensor_tensor(out=hT[:, dc, :], in0=psum[bk][:], in1=hT[:, dc, :],
                                                      op=ALU.add),
                     reads=[b_ps[bk], b_hT[dc]], writes=[b_hT[dc]])
            else:
                P.op("dve", lambda e: e.scalar_tensor_tensor(
                    out=hT[:, dc, :], in0=psum[bk][:], scalar=G(scale_col), in1=hT[:, dc, :],
                    op0=ALU.mult, op1=ALU.add),
                    reads=[b_ps[bk], b_hT[dc], b_cst], writes=[b_hT[dc]])

        def pool_layer(L, T):
            j = L // 2
            norm_stats()
            bX, bY, bZ = reg1.switch(["pX", "pY", "pZ"])
            XW = 4 * 528
            views = [R1[:, i * 2 * XW:(i + 1) * 2 * XW].bitcast(F32).rearrange("p (k t) -> p k t", k=4)
                     for i in range(3)]
            X, Y, Z = views
            for g in range(4):
                win = WINS[g]
                c0 = 4 * g
                P.op("dve", lambda e, c0=c0: e.tensor_copy(out=X[:, :, 0:16], in_=halo[L][:, c0:c0 + 4, :]),
                     reads=[b_halo[L]], writes=[bX])
                for k in range(4):
                    c = c0 + k
                    P.op("dve", lambda e, k=k, c=c: e.scalar_tensor_tensor(
                        out=X[:, k, 16:528], in0=hT[:, c, :], scalar=G(16 * L + c), in1=rstd[:],
                        op0=ALU.mult, op1=ALU.mult),
                        reads=[b_hT[c], b_rstd, b_cst], writes=[bX])
                P.op("dve", lambda e, c0=c0: e.tensor_copy(out=halo[L][:, c0:c0 + 4, :], in_=X[:, :, 512:528]),
                     reads=[bX], writes=[b_halo[L]])
                src, bsrc = X, bX
                pp = [(Y, bY), (Z, bZ)]
                sh = 1
                lo = 1
                n = 0
                while sh < win:
                    dst, bdst = pp[n % 2]
                    P.op("dve", lambda e, dst=dst, src=src, lo=lo, sh=sh: e.tensor_tensor(
                        out=dst[:, :, lo:528], in0=src[:, :, lo:528], in1=src[:, :, lo - sh:528 - sh],
                        op=ALU.add), reads=[bsrc], writes=[bdst])
                    src, bsrc = dst, bdst
                    sh *= 2
                    lo = 2 * sh - 1
                    n += 1
                Sm, bS = src, bsrc
                P.op("dve", lambda e, Sm=Sm, c0=c0, win=win: e.scalar_tensor_tensor(
                    out=hn[:, c0:c0 + 4, :], in0=Sm[:, :, 16:528], scalar=1.0 / win, in1=X[:, :, 16:528],
                    op0=ALU.mult, op1=ALU.subtract),
                    reads=[bS, bX], writes=b_hn[c0:c0 + 4])
                if T == 0:
                    for t in range(win - 1):
                        P.op("dve", lambda e, Sm=Sm, c0=c0, t=t: e.scalar_tensor_tensor(
                            out=hn[:, c0:c0 + 4, t:t + 1], in0=Sm[:, :, 16 + t:17 + t], scalar=1.0 / (t + 1),
                            in1=X[:, :, 16 + t:17 + t], op0=ALU.mult, op1=ALU.subtract),
                            reads=[bS, bX], writes=b_hn[c0:c0 + 4])
            a, wb, _, _ = wsrc[f"pw{j}"]
            for half in range(2):
                wv, wbuf = W.get(wb, 0, 16, half * SLABW, SLABW, conv_tok.get(f"pw{j}"))
                for gg in range(4):
                    for jj in range(2):
                        dc = 4 * gg + 2 * half + jj
                        bk = nbank()
                        fns = [lambda e, k=k, gg=gg, jj=jj, bk=bk, wv=wv: e.matmul(
                            psum[bk][:], lhsT=wv[:, 4 * gg + k, jj * 128:(jj + 1) * 128], rhs=hn[:, 4 * gg + k, :],
                            start=(k == 0), stop=(k == 3)) for k in range(4)]
                        P.pe_group(fns, reads=[wbuf] + b_hn[4 * gg:4 * gg + 4], writes=[b_ps[bk]])
                        resid_add(dc, bk, scale_col=144 + 16 * j + dc)

        def ffn_layer(L):
            norm_stats()
            normalize(64 + 16 * L)
            bact = reg1.switch([f"act{f}" for f in range(NFC)])
            act = R1[:].rearrange("p (f t) -> p f t", t=TT)
            _, wgu, _, _ = wsrc[f"wgu{L}"]
            _, wdn, _, _ = wsrc[f"wdn{L}"]
            cgu = conv_tok.get(f"wgu{L}")
            cdn = conv_tok.get(f"wdn{L}")
            for fp in range(22):
                bg = [nbank(), nbank()]
                bu = [nbank(), nbank()]
                for (col0, bks) in ((fp * SLABW, bg), (FF + fp * SLABW, bu)):
                    wv, wbuf = W.get(wgu, 0, 16, col0, SLABW, cgu)
                    fns = []
                    for jj in range(2):
                        fns += [lambda e, k=k, jj=jj, wv=wv, bk=bks[jj]: e.matmul(
                            psum[bk][:], lhsT=wv[:, k, jj * 128:(jj + 1) * 128], rhs=hn[:, k, :],
                            start=(k == 0), stop=(k == NCH - 1)) for k in range(NCH)]
                    P.pe_group(fns, reads=[wbuf] + b_hn, writes=[b_ps[bks[0]], b_ps[bks[1]]])
                for jj in range(2):
                    fc = 2 * fp + jj
                    ti = ntmp()
                    P.op("act", lambda e, ti=ti, bk=bg[jj]: e.activation(out=tmpf[ti][:], in_=psum[bk][:],
                                                                        func=AF.Silu),
                         reads=[b_ps[bg[jj]]], writes=[b_tmpf[ti]])
                    P.op("dve", lambda e, ti=ti, bk=bu[jj], fc=fc: e.tensor_tensor(
                        out=act[:, fc, :], in0=tmpf[ti][:], in1=psum[bk][:], op=ALU.mult),
                        reads=[b_tmpf[ti], b_ps[bu[jj]]], writes=[bact[fc]])
            for dp in range(8):
                bks = [nbank(), nbank()]
                for fh in range(2):
                    wv, wbuf = W.get(wdn, fh * 2816, 22, dp * SLABW, SLABW, cdn)
                    fns = []
                    for k in range(22):
                        for jj in range(2):
                            fns.append(lambda e, k=k, jj=jj, fh=fh, wv=wv, bk=bks[jj]: e.matmul(
                                psum[bk][:], lhsT=wv[:, k, jj * 128:(jj + 1) * 128], rhs=act[:, fh * 22 + k, :],
                                start=(fh == 0 and k == 0), stop=(fh == 1 and k == 21)))
                    P.pe_group(fns, reads=[wbuf] + bact[fh * 22:(fh + 1) * 22],
                               writes=[b_ps[bks[0]], b_ps[bks[1]]])
                for jj in range(2):
                    resid_add(2 * dp + jj, bks[jj])

        def fox_layer(L, T):
            j = L // 2
            norm_stats()
            normalize(16 * L)
            (bQT,) = reg1.switch(["QT"])
            QT = R1[:, 0:NH * TT].rearrange("p (h t) -> p h t", t=TT)
            _, win, _, _ = wsrc[f"win{j}"]
            _, wout, _, _ = wsrc[f"wout{j}"]
            cin = conv_tok.get(f"win{j}")
            cout = conv_tok.get(f"wout{j}")
            zf = small[:, 0:128].rearrange("p (t h) -> p t h", t=4)
            sp_ = small[:, 128:256].rearrange("p (t h) -> p t h", t=4)
            lc = small[:, 256:384].rearrange("p (t h) -> p t h", t=4)
            r1 = small[:, 384:512].rearrange("p (t h) -> p t h", t=4)
            ss4 = small[:, 512:516]
            t2 = small[:, 1024:1280].rearrange("p (h d) -> p h d", h=4)
            sq = small[:, 1280:1536]
            CH = smallb[:, 0:128].rearrange("p (t h) -> p t h", t=4)
            CM = smallb[:, 128:256].rearrange("p (t h) -> p t h", t=4)
            CL = smallb[:, 256:384].rearrange("p (t h) -> p t h", t=4)
            NHh = smallb[:, 384:512].rearrange("p (t h) -> p t h", t=4)
            NM = smallb[:, 512:640].rearrange("p (t h) -> p t h", t=4)
            NL = smallb[:, 640:768].rearrange("p (t h) -> p t h", t=4)
            bsm = b_small
            fv, fb = W.get(win, 0, 16, 4 * D, NH, cin)
            for tb in range(4):
                bk = nbank()
                fns = [lambda e, k=k, tb=tb, bk=bk: e.matmul(
                    psum[bk][:, 0:NH], lhsT=hn[:, k, tb * 128:(tb + 1) * 128], rhs=fv[:, k, :],
                    start=(k == 0), stop=(k == NCH - 1)) for k in range(NCH)]
                P.pe_group(fns, reads=[fb] + b_hn, writes=[b_ps[bk]])
                P.op("dve", lambda e, tb=tb, bk=bk: e.tensor_tensor(
                    out=zf[:, tb, :], in0=psum[bk][:, 0:NH], in1=cst_sb[:, C_BF + NH * j:C_BF + NH * (j + 1)],
                    op=ALU.add), reads=[b_ps[bk], b_cst], writes=[bsm])
            P.op("act", lambda e: e.activation(out=small[:, 0:128], in_=small[:, 0:128], func=AF.Exp, scale=-1.0),
                 reads=[bsm], writes=[bsm])
            P.op("act", lambda e: e.activation(out=small[:, 128:256], in_=small[:, 0:128], func=AF.Ln, bias=1.0),
                 reads=[bsm], writes=[bsm])
            for tb in range(4):
                bk = nbank()
                fns = [lambda e, t2_=t2_, bk=bk, tb=tb: e.matmul(
                    psum[bk][:, 0:NH], lhsT=(negones[:] if t2_ < tb else cst_sb[:, C_NTRI:C_NTRI + 128]),
                    rhs=sp_[:, t2_, :], start=(t2_ == 0), stop=(t2_ == tb)) for t2_ in range(tb + 1)]
                P.pe_group(fns, reads=[bsm, b_c2, b_cst], writes=[b_ps[bk]])
                P.op("dve", lambda e, tb=tb, bk=bk: e.tensor_copy(out=lc[:, tb, :], in_=psum[bk][:, 0:NH]),
                     reads=[b_ps[bk]], writes=[bsm])
            bk = nbank()
            fns = [lambda e, t2_=t2_, bk=bk: e.matmul(psum[bk][:, 0:NH], lhsT=negones[:], rhs=sp_[:, t2_, :],
                                                       start=(t2_ == 0), stop=(t2_ == 3)) for t2_ in range(4)]
            P.pe_group(fns, reads=[bsm, b_c2], writes=[b_ps[bk]])
            P.op("dve", lambda e, bk=bk: e.tensor_copy(out=TOT[L][:, T, :], in_=psum[bk][:, 0:NH]),
                 reads=[b_ps[bk]], writes=[b_TOT[L]])
            lcf = small[:, 256:384]
            r1f = small[:, 384:512]
            P.op("dve", lambda e: e.tensor_copy(out=smallb[:, 0:128], in_=lcf), reads=[bsm], writes=[bsm])
            P.op("dve", lambda e: e.tensor_tensor(out=r1f, in0=lcf, in1=smallb[:, 0:128], op=ALU.subtract),
                 reads=[bsm], writes=[bsm])
            P.op("dve", lambda e: e.tensor_copy(out=smallb[:, 128:256], in_=r1f), reads=[bsm], writes=[bsm])
            P.op("dve", lambda e: e.tensor_tensor(out=r1f, in0=r1f, in1=smallb[:, 128:256], op=ALU.subtract),
                 reads=[bsm], writes=[bsm])
            P.op("dve", lambda e: e.tensor_copy(out=smallb[:, 256:384], in_=r1f), reads=[bsm], writes=[bsm])
            P.op("dve", lambda e: e.tensor_scalar(out=smallb[:, 384:768], in0=smallb[:, 0:384], scalar1=-1.0,
                                                  scalar2=None, op0=ALU.mult), reads=[bsm], writes=[bsm])
            P.op("dve", lambda e: e.memset(DTAB[:, T, :], 0.0), writes=[b_DT])
            for i in range(T - 1, -1, -1):
                P.op("dve", lambda e, i=i: e.tensor_tensor(out=DTAB[:, i, :], in0=DTAB[:, i + 1, :],
                                                           in1=TOT[L][:, i, :], op=ALU.add),
                     reads=[b_DT, b_TOT[L]], writes=[b_DT])

            def qk_slab(which, sq_i):
                col0 = (0 if which == "q" else D) + sq_i * SLABW
                wv, wbuf = W.get(win, 0, 16, col0, SLABW, cin)
                aug, baug = (augq, b_augq) if which == "q" else (augk, b_augk)
                gain = qg_s[:, 64 * j:64 * (j + 1)] if which == "q" else cst_sb[:, C_KG + 64 * j:C_KG + 64 * (j + 1)]
                h0 = 4 * sq_i
                cs = (CH, CM, CL) if which == "q" else (NHh, NM, NL)
                cbase = 64 if which == "q" else 67
                for n_, csrc in enumerate(cs):
                    P.op("pool", lambda e, csrc=csrc, n_=n_, aug=aug: e.tensor_copy(
                        out=aug[:, :, :, cbase + n_:cbase + n_ + 1], in_=csrc[:, :, h0:h0 + 4].unsqueeze(3)),
                        reads=[bsm], writes=[baug])
                for tb in range(4):
                    bk = nbank()
                    fns = [lambda e, k=k, tb=tb, bk=bk: e.matmul(
                        psum[bk][:, 0:SLABW], lhsT=hn[:, k, tb * 128:(tb + 1) * 128], rhs=wv[:, k, :],
                        start=(k == 0), stop=(k == NCH - 1)) for k in range(NCH)]
                    P.pe_group(fns, reads=[wbuf] + b_hn, writes=[b_ps[bk]])
                    P.op("act", lambda e, bk=bk: e.activation(out=sq, in_=psum[bk][:, 0:SLABW], func=AF.Square),
                         reads=[b_ps[bk]], writes=[bsm])
                    P.op("dve", lambda e: e.tensor_reduce(out=ss4, in_=sq.rearrange("p (h d) -> p h d", h=4),
                                                          axis=AX.X, op=ALU.add), reads=[bsm], writes=[bsm])
                    P.op("act", lambda e: e.activation(out=ss4, in_=ss4, func=AF.Sqrt, bias=EPS, scale=1.0 / HD),
                         reads=[bsm], writes=[bsm])
                    P.op("dve", lambda e: e.reciprocal(out=ss4, in_=ss4), reads=[bsm], writes=[bsm])
                    P.op("dve", lambda e, bk=bk: e.tensor_tensor(
                        out=t2, in0=psum[bk][:, 0:SLABW].rearrange("p (h d) -> p h d", h=4),
                        in1=ss4.unsqueeze(2).broadcast_to([128, 4, HD]), op=ALU.mult),
                        reads=[b_ps[bk], bsm], writes=[bsm])
                    P.op("dve", lambda e, tb=tb, aug=aug: e.tensor_tensor(
                        out=aug[:, tb, :, 0:HD], in0=t2, in1=gain.unsqueeze(1).broadcast_to([128, 4, HD]),
                        op=ALU.mult), reads=[bsm, b_c2, b_cst], writes=[baug])
                if which == "k":
                    ksi = rr["ks"]
                    rr["ks"] ^= 1
                    dstT, bdst = KS[ksi][:].rearrange("p (h t) -> p h t", t=TT), b_KS[ksi]
                for hh in range(4):
                    bk = nbank()
                    pbf = psum[bk][:].bitcast(BF16)
                    fns = [lambda e, tb=tb, hh=hh, pbf=pbf, aug=aug: e.transpose(
                        out=pbf[0:72, tb * 128:(tb + 1) * 128], in_=aug[:, tb, hh, :], identity=ident_bf[:])
                        for tb in range(4)]
                    P.pe_group(fns, reads=[baug, b_c2], writes=[b_ps[bk]])
                    if which == "q":
                        o_ap, ob = QT[0:72, h0 + hh, :], bQT
                    else:
                        o_ap, ob = dstT[0:72, hh, :], bdst
                    eng = alt()
                    if eng == "act":
                        P.op("act", lambda e, o_ap=o_ap, pbf=pbf: e.copy(out=o_ap, in_=pbf[0:72, 0:TT]),
                             reads=[b_ps[bk]], writes=[ob])
                    else:
                        P.op("dve", lambda e, o_ap=o_ap, pbf=pbf: e.tensor_copy(out=o_ap, in_=pbf[0:72, 0:TT]),
                             reads=[b_ps[bk]], writes=[ob])
                if which == "k":
                    return P.op("sp", lambda e, ksi=ksi: e.dma_start(out=kc[L][T, sq_i, :, :], in_=KS[ksi][0:72, :]),
                                reads=[bdst], lane=f"ks{ksi}")
                return None

            for s_ in range(8):
                qk_slab("q", s_)
            kst_tok = [qk_slab("k", s_) for s_ in range(8)]
            vst_tok = []
            for s_ in range(8):
                wv, wbuf = W.get(win, 0, 16, 2 * D + s_ * SLABW, SLABW, cin)
                vsi = rr["vs"]
                rr["vs"] ^= 1
                vv = VS[vsi][:].rearrange("p (a b c) -> p a b c", a=4, b=2)
                for tb in range(4):
                    bk = nbank()
                    fns = [lambda e, k=k, tb=tb, bk=bk, wv=wv: e.matmul(
                        psum[bk][:, 0:SLABW], lhsT=hn[:, k, tb * 128:(tb + 1) * 128], rhs=wv[:, k, :],
                        start=(k == 0), stop=(k == NCH - 1)) for k in range(NCH)]
                    P.pe_group(fns, reads=[wbuf] + b_hn, writes=[b_ps[bk]])
                    pv = psum[bk][:, 0:SLABW].rearrange("p (a b d) -> p a b d", a=2, b=2)
                    for par in range(2):
                        eng = alt()
                        o_ap = vv[:, tb, :, 128 * par:128 * par + 64]
                        i_ap = pv[:, :, par, :]
                        if eng == "act":
                            P.op("act", lambda e, o_ap=o_ap, i_ap=i_ap: e.copy(out=o_ap, in_=i_ap),
                                 reads=[b_ps[bk]], writes=[b_VS[vsi]])
                        else:
                            P.op("dve", lambda e, o_ap=o_ap, i_ap=i_ap: e.tensor_copy(out=o_ap, in_=i_ap),
                                 reads=[b_ps[bk]], writes=[b_VS[vsi]])
                vst_tok.append(P.op("sp", lambda e, vsi=vsi, s_=s_: e.dma_start(out=vc[L][T, s_, :, :], in_=VS[vsi][:]),
                                    reads=[b_VS[vsi]], lane=f"vs{vsi}"))
            for s_ in range(8):
                wv, wbuf = W.get(win, 0, 16, 3 * D + s_ * SLABW, SLABW, cin)
                for jj in range(2):
                    bk = nbank()
                    fns = [lambda e, k=k, jj=jj, bk=bk, wv=wv: e.matmul(
                        psum[bk][:], lhsT=wv[:, k, jj * 128:(jj + 1) * 128], rhs=hn[:, k, :],
                        start=(k == 0), stop=(k == NCH - 1)) for k in range(NCH)]
                    P.pe_group(fns, reads=[wbuf] + b_hn, writes=[b_ps[bk]])
                    P.op("act", lambda e, bk=bk, ch=2 * s_ + jj: e.activation(out=OG[:, ch, :], in_=psum[bk][:],
                                                                             func=AF.Sigmoid),
                         reads=[b_ps[bk]], writes=[b_OG[2 * s_ + jj]])

            kreqs = []
            vreqs = []
            for hg in range(8):
                for i in range(T + 1):
                    kd = [kst_tok[hg]] if i == T else []
                    vd = [vst_tok[hg]] if i == T else []
                    kreqs.append((lambda t_: t_[0:72, :], kc[L][i, hg, :, :], kd))
                    vreqs.append((lambda t_: t_[:], vc[L][i, hg, :, :], vd))
            if not P.dry:
                KSt.begin(kreqs)
                VSt.begin(vreqs)
                for r_ in range(2):
                    KSt.issue(r_)
                    VSt.issue(r_)
            OB = [0, 1, 2, 3]
            SB = [4, 5, 6]
            BC = 7
            for hg in range(8):
                steps = []
                for i in range(T + 1):
                    for kb in range(4):
                        for hh in range(4):
                            steps.append((i, kb, hh))
                kv = {}
                if not P.dry:
                    for i in range(T + 1):
                        kt, kbuf = KSt.slot(hg * (T + 1) + i)
                        vt, vbuf = VSt.slot(hg * (T + 1) + i)
                        kv[i] = (kt[:].rearrange("p (h t) -> p h t", t=TT), kbuf,
                                 vt[:].rearrange("p (a b c) -> p a b c", a=4, b=2), vbuf)
                else:
                    continue

                def emit_S(n):
                    i, kb, hh = steps[n]
                    h = 4 * hg + hh
                    kt, kbuf, _, _ = kv[i]
                    qlo = kb * 128 if i == T else 0
                    nco = TT - qlo
                    sbk = SB[n % 3]
                    fns = [lambda e: e.matmul(psum[sbk][:, 0:nco], lhsT=kt[0:70, hh, kb * 128:(kb + 1) * 128],
                                              rhs=QT[0:70, h, qlo:TT], start=True, stop=(i != T))]
                    if i == T:
                        fns.append(lambda e: e.matmul(psum[sbk][:, 0:128], lhsT=ident_bf[:], rhs=mask_bf[:],
                                                      start=False, stop=True))
                    P.pe_group(fns, reads=[kbuf, bQT, b_c2], writes=[b_ps[sbk]])

                def emit_E(n):
                    i, kb, hh = steps[n]
                    h = 4 * hg + hh
                    qlo = kb * 128 if i == T else 0
                    nco = TT - qlo
                    sbk = SB[n % 3]
                    pi = n % 3
                    P.op("act", lambda e: e.activation(out=PT[pi][:, 0:nco], in_=psum[sbk][:, 0:nco], func=AF.Exp,
                                                       bias=DTAB[:, i, h:h + 1], scale=1.0),
                         reads=[b_ps[sbk], b_DT], writes=[b_PT[pi]])

                def emit_PV(n):
                    i, kb, hh = steps[n]
                    _, _, vt, vbuf = kv[i]
                    qlo = kb * 128 if i == T else 0
                    nco = TT - qlo
                    pi = n % 3
                    obk = OB[hh]
                    pr = hh // 2
                    first = (i == 0 and kb == 0)
                    last = (i == T and kb == 3)
                    if hh % 2 == 0:
                        fn = lambda e: e.matmul(psum[obk][0:65, qlo:TT], lhsT=vt[:, kb, pr, 0:65],
                                                rhs=PT[pi][:, 0:nco], start=first, stop=last)
                    else:
                        fn = lambda e: e.matmul(psum[obk][:, qlo:TT], lhsT=vt[:, kb, pr, 64:192],
                                                rhs=PT[pi][:, 0:nco], start=first, stop=last)
                    P.pe_group([fn], reads=[vbuf, b_PT[pi]], writes=[b_ps[obk]])

                ns = len(steps)
                emit_S(0)
                if ns > 1:
                    emit_S(1)
                for n in range(ns):
                    emit_E(n)
                    if n + 2 < ns:
                        emit_S(n + 2)
                    emit_PV(n)
                    if n == ns - 1 or steps[n + 1][0] != steps[n][0]:
                        r_ = hg * (T + 1) + steps[n][0]
                        KSt.issue(r_ + 2)
                        VSt.issue(r_ + 2)
                for pr in range(2):
                    obe, obo = OB[2 * pr], OB[2 * pr + 1]
                    ch = 2 * hg + pr
                    P.op("act", lambda e, obe=obe: e.copy(out=RR[64:65, :], in_=psum[obe][64:65, :]),
                         reads=[b_ps[obe]], writes=[b_RR])
                    P.op("act", lambda e, obo=obo: e.copy(out=RR[0:1, :], in_=psum[obo][0:1, :]),
                         reads=[b_ps[obo]], writes=[b_RR])
                    P.pe_group([lambda e: e.matmul(psum[BC][:], lhsT=cst_sb[:, C_SEL:C_SEL + 128], rhs=RR[:],
                                                   start=True, stop=True)],
                               reads=[b_RR, b_cst], writes=[b_ps[BC]])
                    P.op("dve", lambda e: e.reciprocal(out=RC[:], in_=psum[BC][:]), reads=[b_ps[BC]], writes=[b_RC])
                    P.op("dve", lambda e, ch=ch: e.tensor_tensor(out=RC[:], in0=RC[:], in1=OG[:, ch, :], op=ALU.mult),
                         reads=[b_RC, b_OG[ch]], writes=[b_RC])
                    P.op("dve", lambda e, ch=ch, obe=obe: e.tensor_tensor(
                        out=OG[0:64, ch, :], in0=psum[obe][0:64, :], in1=RC[0:64, :], op=ALU.mult),
                        reads=[b_ps[obe], b_RC], writes=[b_OG[ch]])
                    P.op("dve", lambda e, ch=ch, obo=obo: e.tensor_tensor(
                        out=OG[64:128, ch, :], in0=psum[obo][64:128, :], in1=RC[64:128, :], op=ALU.mult),
                        reads=[b_ps[obo], b_RC], writes=[b_OG[ch]])
            rr["bank"] = 0
            for s_ in range(8):
                wv, wbuf = W.get(wout, 0, 16, s_ * SLABW, SLABW, cout)
                for jj in range(2):
                    bk = nbank()
                    fns = [lambda e, k=k, jj=jj, bk=bk, wv=wv: e.matmul(
                        psum[bk][:], lhsT=wv[:, k, jj * 128:(jj + 1) * 128], rhs=OG[:, k, :],
                        start=(k == 0), stop=(k == NCH - 1)) for k in range(NCH)]
                    P.pe_group(fns, reads=[wbuf] + b_OG, writes=[b_ps[bk]])
                    resid_add(2 * s_ + jj, bk)

        def store_out(T):
            if do_final:
                norm_stats()
                gcol = 128
            (b_os,) = reg1.switch(["ostage"])
            os_ = R1[:, 0:16384].bitcast(F32).rearrange("p (t d) -> p t d", t=4)
            fin = R1[:, 16384:16384 + 1024].bitcast(F32)
            (b_fin,) = [Buf("fin")]
            b_fin.r = list(b_os.r)
            for c in range(NCH):
                if do_final:
                    P.op("dve", lambda e, c=c: e.scalar_tensor_tensor(
                        out=fin, in0=hT[:, c, :], scalar=G(gcol + c), in1=rstd[:], op0=ALU.mult, op1=ALU.mult),
                        reads=[b_hT[c], b_rstd, b_cst], writes=[b_fin])
                    src_ap, src_b = fin, b_fin
                else:
                    src_ap, src_b = hT[:, c, :], b_hT[c]
                bk = nbank()
                fns = [lambda e, tb=tb, bk=bk, src_ap=src_ap: e.transpose(
                    out=psum[bk][:, tb * 128:(tb + 1) * 128], in_=src_ap[:, tb * 128:(tb + 1) * 128],
                    identity=cst_sb[:, C_ID:C_ID + 128]) for tb in range(4)]
                P.pe_group(fns, reads=[src_b, b_cst], writes=[b_ps[bk]])
                eng = alt()
                o_ap = os_[:, :, c * 128:(c + 1) * 128]
                i_ap = psum[bk][:].rearrange("p (t d) -> p t d", t=4)
                if eng == "act":
                    P.op("act", lambda e, o_ap=o_ap, i_ap=i_ap: e.copy(out=o_ap, in_=i_ap),
                         reads=[b_ps[bk]], writes=[b_os])
                else:
                    P.op("dve", lambda e, o_ap=o_ap, i_ap=i_ap: e.tensor_copy(out=o_ap, in_=i_ap),
                         reads=[b_ps[bk]], writes=[b_os])
            dst = out[T * TT:(T + 1) * TT, :].rearrange("(t p) d -> p t d", p=128)
            return P.op("sp", lambda e: e.dma_start(out=dst, in_=os_), reads=[b_os], lane="o")

        def walk():
            last = None
            for T in range(NT):
                load_x(T)
                for L in layers:
                    if L % 2 == 0:
                        pool_layer(L, T)
                    else:
                        fox_layer(L, T)
                    ffn_layer(L)
                last = store_out(T)
            return last

        emit_init()
        P.dry = True
        walk()
        P.dry = False
        W.pos = 0
        for k_ in rr:
            rr[k_] = 0
        reg1.live = []
        last = walk()
        P.lists["sp"].append(([(last.sem, last.val)], None, None, 0))

        with nc.Block() as block:
            def runner(eng):
                def f(e):
                    for waits, fn, tok, inc in P.lists[eng]:
                        for s_, v_ in waits:
                            e.wait_ge(s_, v_)
                        if fn is None:
                            continue
                        ins = fn(e)
                        if tok is not None:
                            ins.then_inc(tok.sem, inc)
                return f

            block.tensor(runner("pe"))
            block.scalar(runner("act"))
            block.vector(runner("dve"))
            block.gpsimd(runner("pool"))
            block.sync(runner("sp"))
    return nc


def make_consts(inputs):
    c = np.zeros((128, CW), np.float32)
    c[:, C_ID:C_ID + 128] = np.eye(128, dtype=np.float32)
    r = np.arange(128)
    c[:, C_NTRI:C_NTRI + 128] = -(r[:, None] <= r[None, :]).astype(np.float32)
    c[:, C_MASK:C_MASK + 128] = np.where(r[:, None] > r[None, :], MASKV, 0.0).astype(np.float32)
    c[64, C_SEL:C_SEL + 64] = 1.0
    c[0, C_SEL + 64:C_SEL + 128] = 1.0

    def fm(v):
        return np.asarray(v, np.float32).reshape(16, 128).T

    for L in range(4):
        c[:, C_G + 16 * L:C_G + 16 * (L + 1)] = fm(inputs["attn_norm_g"][L])
        c[:, C_G + 64 + 16 * L:C_G + 64 + 16 * (L + 1)] = fm(inputs["ffn_norm_g"][L])
    c[:, C_G + 128:C_G + 144] = fm(inputs["final_norm_g"])
    for j in range(2):
        c[:, C_G + 144 + 16 * j:C_G + 160 + 16 * j] = fm(inputs["pool_scale"][j])
        c[:, C_BF + 32 * j:C_BF + 32 * (j + 1)] = np.asarray(inputs["fox_b_f"][j], np.float32)[None, :]
        c[:, C_QG + 64 * j:C_QG + 64 * (j + 1)] = np.asarray(inputs["fox_q_norm_g"][j], np.float32)[None, :]
        c[:, C_KG + 64 * j:C_KG + 64 * (j + 1)] = np.asarray(inputs["fox_k_norm_g"][j], np.float32)[None, :]
    return c


def weight_map(inputs, layers):
    m = {}
    for L in layers:
        j = L // 2
        if L % 2 == 0:
            m[f"pw{j}"] = np.ascontiguousarray(np.asarray(inputs["pool_w"][j], np.float32).reshape(2048, 512))
        else:
            m[f"win{j}"] = np.ascontiguousarray(np.asarray(inputs["fox_w_in"][j], np.float32))
            m[f"wout{j}"] = np.ascontiguousarray(np.asarray(inputs["fox_w_out"][j], np.float32))
        m[f"wgu{L}"] = np.ascontiguousarray(np.asarray(inputs["ffn_w_gate_up"][L], np.float32))
        m[f"wdn{L}"] = np.ascontiguousarray(np.asarray(inputs["ffn_w_down"][L], np.float32))
    return m


def run_layers(xs, inputs, layers, do_final, NT):
    nc = build(NT, layers, do_final)
    cst = make_consts(inputs)
    wm = weight_map(inputs, layers)
    in_maps = []
    for xb in xs:
        d = {"x": np.ascontiguousarray(xb, dtype=np.float32), "cst": cst}
        d.update(wm)
        in_maps.append(d)
    res = run_bass_kernel_spmd(nc, in_maps, core_ids=list(range(len(xs))))
    return [np.asarray(r["out"]) for r in res.results]


FUSED = True


def kernel(**inputs):
    x = np.asarray(inputs["x"], np.float32)
    B, S, _ = x.shape
    NT = S // TT
    xs = [x[b] for b in range(B)]
    if FUSED:
        outs = run_layers(xs, inputs, [0, 1, 2, 3], True, NT)
    else:
        for L in range(4):
            xs = run_layers(xs, inputs, [L], L == 3, NT)
        outs = xs
    return np.stack(outs, axis=0).astype(np.float32)
```

```python
from contextlib import ExitStack

import numpy as np
import concourse.bass as bass
import concourse.mybir as mybir
from concourse.bass_utils import run_bass_kernel_spmd

F32 = mybir.dt.float32
BF16 = mybir.dt.bfloat16
AF = mybir.ActivationFunctionType
ALU = mybir.AluOpType
AX = mybir.AxisListType

D = 2048
NCH = 16
TT = 512
FF = 5632
NFC = 44
NH = 32
HD = 64
FOX_IN = 4 * D + NH
EPS = 1e-6
WINS = (2, 4, 8, 16)
SLABW = 256
SLOT_COLS = 22 * SLABW
NSLOT = 3
MASKV = -30000.0
SEM_MAX = 4000

C_ID = 0
C_NTRI = 128
C_MASK = 256
C_SEL = 384
C_G = 512
C_BF = C_G + 176
C_QG = C_BF + 64
C_KG = C_QG + 128
CW = C_KG + 128


class Tok:
    __slots__ = ("eng", "sem", "val")

    def __init__(self, eng, sem, val):
        self.eng, self.sem, self.val = eng, sem, val


class Buf:
    def __init__(self, name, const=False):
        self.name = name
        self.w = None
        self.r = []
        self.const = const


class Prog:
    ENGS = ("pe", "act", "dve", "pool", "sp")

    def __init__(self, nc, stack):
        self.nc = nc
        self.stack = stack
        self.lists = {e: [] for e in self.ENGS}
        self.cur = {}
        self.lanes = {}
        self.waited = {e: {} for e in self.ENGS}
        self.nsem = 0
        self.dry = False

    def newsem(self):
        s = self.stack.enter_context(self.nc.semaphore(f"sm{self.nsem}"))
        self.nsem += 1
        return s

    def _tok(self, eng):
        c = self.cur.get(eng)
        if c is None or c[1] >= SEM_MAX:
            c = [self.newsem(), 0]
            self.cur[eng] = c
        c[1] += 1
        return Tok(eng, c[0], c[1])

    def _lane(self, eng, lane):
        c = self.lanes.get(lane)
        if c is None or c[1] >= SEM_MAX:
            c = [self.newsem(), 0]
            self.lanes[lane] = c
        c[1] += 16
        return Tok("dma", c[0], c[1])

    def _waits(self, eng, reads, writes, deps):
        d = [t for t in deps if t is not None]
        for b in reads:
            if b.w is not None:
                d.append(b.w)
        for b in writes:
            if b.w is not None:
                d.append(b.w)
            d.extend(t for t in b.r if t is not None)
        best = {}
        for t in d:
            if eng == "pe" and t.eng == "pe":
                continue
            k = id(t.sem)
            if self.waited[eng].get(k, -1) >= t.val:
                continue
            if k not in best or best[k][1] < t.val:
                best[k] = (t.sem, t.val)
        for k, (s, v) in best.items():
            self.waited[eng][k] = v
        return list(best.values())

    def _note(self, tok, reads, writes):
        for b in reads:
            if not b.const:
                b.r.append(tok)
        for b in writes:
            b.w = tok
            b.r = []

    def op(self, eng, fn, reads=(), writes=(), deps=(), lane=None):
        if self.dry:
            return None
        waits = self._waits(eng, reads, writes, deps)
        if lane is not None:
            tok, inc = self._lane(eng, lane), 16
        else:
            tok, inc = self._tok(eng), 1
        self.lists[eng].append((waits, fn, tok, inc))
        self._note(tok, reads, writes)
        return tok

    def pe_group(self, fns, reads=(), writes=(), deps=()):
        if self.dry:
            return None
        waits = self._waits("pe", reads, writes, deps)
        tok = self._tok("pe")
        n = len(fns)
        for i, fn in enumerate(fns):
            self.lists["pe"].append((waits if i == 0 else [], fn, tok if i == n - 1 else None, 1))
        self._note(tok, reads, writes)
        return tok

    def run(self, eng, e):
        for waits, fn, tok, inc in self.lists[eng]:
            for s, v in waits:
                e.wait_ge(s, v)
            ins = fn(e)
            if tok is not None:
                ins.then_inc(tok.sem, inc)


def fence_of(bufs):
    toks = []
    for b in bufs:
        if b.w is not None:
            toks.append(b.w)
        toks.extend(t for t in b.r if t is not None)
    best = {}
    for t in toks:
        k = id(t.sem)
        if k not in best or best[k].val < t.val:
            best[k] = t
    return list(best.values())


class Region:
    def __init__(self):
        self.live = []

    def switch(self, names):
        f = fence_of(self.live)
        bufs = []
        for n in names:
            b = Buf(n)
            b.r = list(f)
            bufs.append(b)
        self.live = bufs
        return bufs


class WStream:
    def __init__(self, P, slots):
        self.P = P
        self.slots = slots
        self.reqs = []
        self.pos = 0
        self.issued = 0
        self.hook = None
        self.conv_of = None

    def _view(self, slot_t, nk, ncols):
        return slot_t[:, 0:nk * ncols].rearrange("p (k c) -> p k c", c=ncols)

    def get(self, src2d, r0, nk, c0, ncols, conv):
        i = self.pos
        self.pos += 1
        slot_t, slot_b = self.slots[i % len(self.slots)]
        view = self._view(slot_t, nk, ncols)
        if self.P.dry:
            self.reqs.append((src2d, r0, nk, c0, ncols, conv))
            return view, slot_b
        if self.hook is not None:
            self.hook()
        while self.issued < min(len(self.reqs), i + len(self.slots)):
            self._issue(self.issued)
            self.issued += 1
        return view, slot_b

    def _issue(self, i):
        src2d, r0, nk, c0, ncols, conv = self.reqs[i]
        conv = self.conv_of(conv)
        slot_t, slot_b = self.slots[i % len(self.slots)]
        dst = self._view(slot_t, nk, ncols)
        src = src2d[r0:r0 + nk * 128, c0:c0 + ncols].rearrange("(k p) c -> p k c", p=128)
        self.P.op("sp", lambda e, dst=dst, src=src: e.dma_start(out=dst, in_=src),
                  writes=[slot_b], deps=[conv], lane=f"w{i % len(self.slots)}")


class KVStream:
    def __init__(self, P, name, slots):
        self.P = P
        self.name = name
        self.slots = slots
        self.n = 0
        self.reqs = []
        self.base = 0

    def begin(self, reqs):
        self.base = self.n
        self.reqs = reqs
        self.n += len(reqs)

    def issue(self, k):
        if k >= len(self.reqs):
            return
        dst_fn, src, deps = self.reqs[k]
        slot_t, slot_b = self.slots[(self.base + k) % len(self.slots)]
        dst = dst_fn(slot_t)
        self.P.op("sp", lambda e, dst=dst, src=src: e.dma_start(out=dst, in_=src),
                  writes=[slot_b], deps=deps, lane=f"{self.name}{(self.base + k) % len(self.slots)}")

    def slot(self, k):
        return self.slots[(self.base + k) % len(self.slots)]


def build(NT, layers, do_final):
    nc = bass.Bass("TRN2", target_bir_lowering=False)
    S = NT * TT
    x = nc.dram_tensor("x", [S, D], F32, kind="ExternalInput").ap()
    out = nc.dram_tensor("out", [S, D], F32, kind="ExternalOutput").ap()
    cst = nc.dram_tensor("cst", [128, CW], F32, kind="ExternalInput").ap()

    wsrc = {}

    def decl_w(name, rows, cols):
        a = nc.dram_tensor(name, [rows, cols], F32, kind="ExternalInput").ap()
        b = nc.dram_tensor(name + "_bf", [rows, cols], BF16, kind="Internal").ap()
        wsrc[name] = (a, b, rows, cols)

    worder = []
    for L in layers:
        j = L // 2
        if L % 2 == 0:
            decl_w(f"pw{j}", 2048, 512)
            worder.append(f"pw{j}")
        else:
            decl_w(f"win{j}", D, FOX_IN)
            decl_w(f"wout{j}", D, D)
            worder += [f"win{j}", f"wout{j}"]
        decl_w(f"wgu{L}", D, 2 * FF)
        decl_w(f"wdn{L}", FF, D)
        worder += [f"wgu{L}", f"wdn{L}"]
    fox_layers = [L for L in layers if L % 2 == 1]
    kc = {}
    vc = {}
    for L in fox_layers:
        kc[L] = nc.dram_tensor(f"kc{L}", [NT, 8, 72, 4 * TT], BF16, kind="Internal").ap()
        vc[L] = nc.dram_tensor(f"vc{L}", [NT, 8, 128, 4 * 2 * 192], BF16, kind="Internal").ap()

    with ExitStack() as st:
        def sb(name, shape, dt):
            return st.enter_context(nc.sbuf_tensor(name, shape, dt))

        P = Prog(nc, st)
        cst_sb = sb("cst_sb", [128, CW], F32)
        hT = sb("hT", [128, NCH, TT], F32)
        hn = sb("hn", [128, NCH, TT], BF16)
        R1 = sb("R1", [128, NFC * TT], BF16)
        OG = sb("OG", [128, NCH, TT], BF16)
        KS = [sb(f"KS{i}", [128, 4 * TT], BF16) for i in range(2)]
        KLs = [sb(f"KL{i}", [128, 4 * TT], BF16) for i in range(2)]
        VS = [sb(f"VS{i}", [128, 4 * 2 * 192], BF16) for i in range(2)]
        VLs = [sb(f"VL{i}", [128, 4 * 2 * 192], BF16) for i in range(2)]
        slots = [sb(f"slab{i}", [128, SLOT_COLS], BF16) for i in range(NSLOT)]
        PT = [sb(f"PT{i}", [128, TT], BF16) for i in range(3)]
        tmpf = [sb(f"tmpf{i}", [128, TT], F32) for i in range(2)]
        rstd = sb("rstd", [128, TT], F32)
        RR = sb("RR", [128, TT], F32)
        RC = sb("RC", [128, TT], F32)
        ident_bf = sb("ident_bf", [128, 128], BF16)
        ones_bf = sb("ones_bf", [128, 128], BF16)
        mask_bf = sb("mask_bf", [128, 128], BF16)
        negones = sb("negones", [128, 128], F32)
        halo = {L: sb(f"halo{L}", [128, NCH, 16], F32) for L in layers if L % 2 == 0}
        augq = sb("augq", [128, 4, 4, 72], BF16)
        augk = sb("augk", [128, 4, 4, 72], BF16)
        small = sb("small", [128, 1536], F32)
        smallb = sb("smallb", [128, 768], BF16)
        TOT = {L: sb(f"TOT{L}", [128, NT, NH], F32) for L in fox_layers}
        DTAB = sb("DTAB", [128, NT, NH], F32)
        qg_s = sb("qg_s", [128, 128], F32)
        psum = [st.enter_context(nc.psum_tensor(f"ps{i}", [128, TT], F32)) for i in range(8)]

        b_cst = Buf("cst", const=True)
        b_hT = [Buf(f"hT{c}") for c in range(NCH)]
        b_hn = [Buf(f"hn{c}") for c in range(NCH)]
        b_OG = [Buf(f"OG{c}") for c in range(NCH)]
        b_ps = [Buf(f"ps{i}") for i in range(8)]
        b_KS = [Buf(f"KS{i}") for i in range(2)]
        b_KL = [Buf(f"KL{i}") for i in range(2)]
        b_VS = [Buf(f"VS{i}") for i in range(2)]
        b_VL = [Buf(f"VL{i}") for i in range(2)]
        b_slots = [Buf(f"slab{i}") for i in range(NSLOT)]
        b_PT = [Buf(f"PT{i}") for i in range(3)]
        b_tmpf = [Buf(f"tmpf{i}") for i in range(2)]
        b_rstd = Buf("rstd")
        b_RR = Buf("RR")
        b_RC = Buf("RC")
        b_c2 = Buf("consts2", const=True)
        b_halo = {L: Buf(f"halo{L}") for L in halo}
        b_augq = Buf("augq")
        b_augk = Buf("augk")
        b_small = Buf("small")
        b_TOT = {L: Buf(f"TOT{L}") for L in fox_layers}
        b_DT = Buf("DTAB")
        reg1 = Region()

        W = WStream(P, list(zip(slots, b_slots)))
        KSt = KVStream(P, "kl", list(zip(KLs, b_KL)))
        VSt = KVStream(P, "vl", list(zip(VLs, b_VL)))

        rr = {"bank": 0, "tmp": 0, "alt": 0, "ks": 0, "vs": 0, "pt": 0}

        def nbank():
            i = rr["bank"]
            rr["bank"] = (i + 1) % 8
            return i

        def ntmp():
            i = rr["tmp"]
            rr["tmp"] = (i + 1) % 2
            return i

        def alt():
            rr["alt"] ^= 1
            return "act" if rr["alt"] else "dve"

        G = lambda col: cst_sb[:, C_G + col:C_G + col + 1]

        conv_tok = {}
        pending = []

        def nchunks(name):
            return {"pw": 1, "wi": 16, "wo": 4, "wg": 16, "wd": 8}[name[:2]]

        def emit_chunk(name, idx):
            a, b, rows, cols = wsrc[name]
            n = nchunks(name)
            rs = rows // n
            tok = P.op("pool", lambda e: e.dma_start(out=b[idx * rs:(idx + 1) * rs, :],
                                                     in_=a[idx * rs:(idx + 1) * rs, :]), lane="cv_" + name)
            if idx == n - 1:
                conv_tok[name] = tok

        def flush(name):
            todo = [p_ for p_ in pending if p_[0] == name]
            for p_ in todo:
                pending.remove(p_)
                emit_chunk(*p_)

        def conv_hook():
            if pending:
                emit_chunk(*pending.pop(0))

        def conv_of(name):
            if name is None:
                return None
            if name not in conv_tok:
                flush(name)
            return conv_tok[name]

        W.hook = conv_hook
        W.conv_of = conv_of

        def emit_init():
            P.op("sp", lambda e: e.dma_start(out=cst_sb[:], in_=cst), writes=[b_cst], lane="cst")
            first = worder[:(3 if layers[0] % 2 == 0 else 4)]
            for name in worder:
                for idx in range(nchunks(name)):
                    pending.append((name, idx))
            for name in first:
                flush(name)
            P.op("dve", lambda e: e.tensor_copy(out=ident_bf[:], in_=cst_sb[:, C_ID:C_ID + 128]),
                 reads=[b_cst], writes=[b_c2])
            P.op("dve", lambda e: e.tensor_copy(out=mask_bf[:], in_=cst_sb[:, C_MASK:C_MASK + 128]),
                 reads=[b_cst], writes=[b_c2])
            P.op("dve", lambda e: e.memset(ones_bf[:], 1.0), writes=[b_c2])
            P.op("dve", lambda e: e.memset(negones[:], -1.0), writes=[b_c2])
            P.op("dve", lambda e: e.memset(RR[:], 0.0), writes=[b_RR])
            P.op("dve", lambda e: e.tensor_scalar(out=qg_s[:], in0=cst_sb[:, C_QG:C_QG + 128],
                                                  scalar1=float(HD ** -0.5), scalar2=None, op0=ALU.mult),
                 reads=[b_cst], writes=[b_c2])
            for L in halo:
                P.op("dve", lambda e, L=L: e.memset(halo[L][:], 0.0), writes=[b_halo[L]])
            for t_, bb in list(zip(VS, b_VS)) + list(zip(VLs, b_VL)):
                v = t_[:].rearrange("p (a b c) -> p a b c", a=4, b=2)
                P.op("dve", lambda e, v=v: e.memset(v[:, :, :, 64:128], 0.0), writes=[bb])
                P.op("dve", lambda e, v=v: e.memset(v[:, :, :, 64:65], 1.0), writes=[bb])
            P.op("dve", lambda e: e.memset(augq[:], 0.0), writes=[b_augq])
            P.op("dve", lambda e: e.memset(augq[:, :, :, 67:70], 1.0), writes=[b_augq])
            P.op("dve", lambda e: e.memset(augk[:], 0.0), writes=[b_augk])
            P.op("dve", lambda e: e.memset(augk[:, :, :, 64:67], 1.0), writes=[b_augk])

        def load_x(T):
            (b_xs,) = reg1.switch(["xstage"])
            xs = R1[:, 0:16384].bitcast(F32).rearrange("p (t d) -> p t d", t=4)
            src = x[T * TT:(T + 1) * TT, :].rearrange("(t p) d -> p t d", p=128)
            P.op("sp", lambda e: e.dma_start(out=xs, in_=src), writes=[b_xs], lane="x")
            for c in range(NCH):
                bk = nbank()
                fns = [lambda e, tb=tb, c=c, bk=bk: e.transpose(
                    out=psum[bk][:, tb * 128:(tb + 1) * 128], in_=xs[:, tb, c * 128:(c + 1) * 128],
                    identity=cst_sb[:, C_ID:C_ID + 128]) for tb in range(4)]
                P.pe_group(fns, reads=[b_xs, b_cst], writes=[b_ps[bk]])
                eng = alt()
                if eng == "act":
                    P.op("act", lambda e, c=c, bk=bk: e.copy(out=hT[:, c, :], in_=psum[bk][:]),
                         reads=[b_ps[bk]], writes=[b_hT[c]])
                else:
                    P.op("dve", lambda e, c=c, bk=bk: e.tensor_copy(out=hT[:, c, :], in_=psum[bk][:]),
                         reads=[b_ps[bk]], writes=[b_hT[c]])

        def norm_stats():
            for c in range(NCH):
                P.op("act", lambda e, c=c: e.activation(out=hn[:, c, :], in_=hT[:, c, :], func=AF.Square),
                     reads=[b_hT[c]], writes=[b_hn[c]])
            bk = nbank()
            fns = [lambda e, c=c, bk=bk: e.matmul(psum[bk][:], lhsT=ones_bf[:], rhs=hn[:, c, :],
                                                  start=(c == 0), stop=(c == NCH - 1)) for c in range(NCH)]
            P.pe_group(fns, reads=b_hn + [b_c2], writes=[b_ps[bk]])
            P.op("act", lambda e, bk=bk: e.activation(out=rstd[:], in_=psum[bk][:], func=AF.Sqrt,
                                                      bias=EPS, scale=1.0 / D),
                 reads=[b_ps[bk]], writes=[b_rstd])
            P.op("dve", lambda e: e.reciprocal(out=rstd[:], in_=rstd[:]), reads=[b_rstd], writes=[b_rstd])

        def normalize(gcol0):
            for c in range(NCH):
                P.op("dve", lambda e, c=c: e.scalar_tensor_tensor(
                    out=hn[:, c, :], in0=hT[:, c, :], scalar=G(gcol0 + c), in1=rstd[:],
                    op0=ALU.mult, op1=ALU.mult),
                    reads=[b_hT[c], b_rstd, b_cst], writes=[b_hn[c]])

        def resid_add(dc, bk, scale_col=None):
            if scale_col is None:
                P.op("dve", lambda e: e.tensor_tensor(out=hT[:, dc, :], in0=psum[bk][:], in1=hT[:, dc, :],
                                                      op=ALU.add),
                     reads=[b_ps[bk], b_hT[dc]], writes=[b_hT[dc]])
            else:
                P.op("dve", lambda e: e.scalar_tensor_tensor(
                    out=hT[:, dc, :], in0=psum[bk][:], scalar=G(scale_col), in1=hT[:, dc, :],
                    op0=ALU.mult, op1=ALU.add),
                    reads=[b_ps[bk], b_hT[dc], b_cst], writes=[b_hT[dc]])

        def pool_layer(L, T):
            j = L // 2
            norm_stats()
            bX, bY, bZ = reg1.switch(["pX", "pY", "pZ"])
            XW = 4 * 528
            views = [R1[:, i * 2 * XW:(i + 1) * 2 * XW].bitcast(F32).rearrange("p (k t) -> p k t", k=4)
                     for i in range(3)]
            X, Y, Z = views
            for g in range(4):
                win = WINS[g]
                c0 = 4 * g
                P.op("dve", lambda e, c0=c0: e.tensor_copy(out=X[:, :, 0:16], in_=halo[L][:, c0:c0 + 4, :]),
                     reads=[b_halo[L]], writes=[bX])
                for k in range(4):
                    c = c0 + k
                    P.op("dve", lambda e, k=k, c=c: e.scalar_tensor_tensor(
                        out=X[:, k, 16:528], in0=hT[:, c, :], scalar=G(16 * L + c), in1=rstd[:],
                        op0=ALU.mult, op1=ALU.mult),
                        reads=[b_hT[c], b_rstd, b_cst], writes=[bX])
                P.op("dve", lambda e, c0=c0: e.tensor_copy(out=halo[L][:, c0:c0 + 4, :], in_=X[:, :, 512:528]),
                     reads=[bX], writes=[b_halo[L]])
                src, bsrc = X, bX
                pp = [(Y, bY), (Z, bZ)]
                sh = 1
                lo = 1
                n = 0
                while sh < win:
                    dst, bdst = pp[n % 2]
                    P.op("dve", lambda e, dst=dst, src=src, lo=lo, sh=sh: e.tensor_tensor(
                        out=dst[:, :, lo:528], in0=src[:, :, lo:528], in1=src[:, :, lo - sh:528 - sh],
                        op=ALU.add), reads=[bsrc], writes=[bdst])
                    src, bsrc = dst, bdst
                    sh *= 2
                    lo = 2 * sh - 1
                    n += 1
                Sm, bS = src, bsrc
                P.op("dve", lambda e, Sm=Sm, c0=c0, win=win: e.scalar_tensor_tensor(
                    out=hn[:, c0:c0 + 4, :], in0=Sm[:, :, 16:528], scalar=1.0 / win, in1=X[:, :, 16:528],
                    op0=ALU.mult, op1=ALU.subtract),
                    reads=[bS, bX], writes=b_hn[c0:c0 + 4])
                if T == 0:
                    for t in range(win - 1):
                        P.op("dve", lambda e, Sm=Sm, c0=c0, t=t: e.scalar_tensor_tensor(
                            out=hn[:, c0:c0 + 4, t:t + 1], in0=Sm[:, :, 16 + t:17 + t], scalar=1.0 / (t + 1),
                            in1=X[:, :, 16 + t:17 + t], op0=ALU.mult, op1=ALU.subtract),
                            reads=[bS, bX], writes=b_hn[c0:c0 + 4])
            a, wb, _, _ = wsrc[f"pw{j}"]
            for half in range(2):
                wv, wbuf = W.get(wb, 0, 16, half * SLABW, SLABW, f"pw{j}")
                for gg in range(4):
                    for jj in range(2):
                        dc = 4 * gg + 2 * half + jj
                        bk = nbank()
                        fns = [lambda e, k=k, gg=gg, jj=jj, bk=bk, wv=wv: e.matmul(
                            psum[bk][:], lhsT=wv[:, 4 * gg + k, jj * 128:(jj + 1) * 128], rhs=hn[:, 4 * gg + k, :],
                            start=(k == 0), stop=(k == 3)) for k in range(4)]
                        P.pe_group(fns, reads=[wbuf] + b_hn[4 * gg:4 * gg + 4], writes=[b_ps[bk]])
                        resid_add(dc, bk, scale_col=144 + 16 * j + dc)

        def ffn_layer(L):
            norm_stats()
            normalize(64 + 16 * L)
            bact = reg1.switch([f"act{f}" for f in range(NFC)])
            act = R1[:].rearrange("p (f t) -> p f t", t=TT)
            _, wgu, _, _ = wsrc[f"wgu{L}"]
            _, wdn, _, _ = wsrc[f"wdn{L}"]
            cgu = f"wgu{L}"
            cdn = f"wdn{L}"
            for fp in range(22):
                bg = [nbank(), nbank()]
                bu = [nbank(), nbank()]
                for (col0, bks) in ((fp * SLABW, bg), (FF + fp * SLABW, bu)):
                    wv, wbuf = W.get(wgu, 0, 16, col0, SLABW, cgu)
                    fns = []
                    for jj in range(2):
                        fns += [lambda e, k=k, jj=jj, wv=wv, bk=bks[jj]: e.matmul(
                            psum[bk][:], lhsT=wv[:, k, jj * 128:(jj + 1) * 128], rhs=hn[:, k, :],
                            start=(k == 0), stop=(k == NCH - 1)) for k in range(NCH)]
                    P.pe_group(fns, reads=[wbuf] + b_hn, writes=[b_ps[bks[0]], b_ps[bks[1]]])
                for jj in range(2):
                    fc = 2 * fp + jj
                    ti = ntmp()
                    P.op("act", lambda e, ti=ti, bk=bg[jj]: e.activation(out=tmpf[ti][:], in_=psum[bk][:],
                                                                        func=AF.Silu),
                         reads=[b_ps[bg[jj]]], writes=[b_tmpf[ti]])
                    P.op("dve", lambda e, ti=ti, bk=bu[jj], fc=fc: e.tensor_tensor(
                        out=act[:, fc, :], in0=tmpf[ti][:], in1=psum[bk][:], op=ALU.mult),
                        reads=[b_tmpf[ti], b_ps[bu[jj]]], writes=[bact[fc]])
            for dp in range(8):
                bks = [nbank(), nbank()]
                for fh in range(2):
                    wv, wbuf = W.get(wdn, fh * 2816, 22, dp * SLABW, SLABW, cdn)
                    fns = []
                    for k in range(22):
                        for jj in range(2):
                            fns.append(lambda e, k=k, jj=jj, fh=fh, wv=wv, bk=bks[jj]: e.matmul(
                                psum[bk][:], lhsT=wv[:, k, jj * 128:(jj + 1) * 128], rhs=act[:, fh * 22 + k, :],
                                start=(fh == 0 and k == 0), stop=(fh == 1 and k == 21)))
                    P.pe_group(fns, reads=[wbuf] + bact[fh * 22:(fh + 1) * 22],
                               writes=[b_ps[bks[0]], b_ps[bks[1]]])
                for jj in range(2):
                    resid_add(2 * dp + jj, bks[jj])

        def fox_layer(L, T):
            j = L // 2
            norm_stats()
            normalize(16 * L)
            (bQT,) = reg1.switch(["QT"])
            QT = R1[:, 0:NH * TT].rearrange("p (h t) -> p h t", t=TT)
            _, win, _, _ = wsrc[f"win{j}"]
            _, wout, _, _ = wsrc[f"wout{j}"]
            cin = f"win{j}"
            cout = f"wout{j}"
            zf = small[:, 0:128].rearrange("p (t h) -> p t h", t=4)
            sp_ = small[:, 128:256].rearrange("p (t h) -> p t h", t=4)
            lc = small[:, 256:384].rearrange("p (t h) -> p t h", t=4)
            r1 = small[:, 384:512].rearrange("p (t h) -> p t h", t=4)
            ss4 = small[:, 512:516]
            t2 = small[:, 1024:1280].rearrange("p (h d) -> p h d", h=4)
            sq = small[:, 1280:1536]
            CH = smallb[:, 0:128].rearrange("p (t h) -> p t h", t=4)
            CM = smallb[:, 128:256].rearrange("p (t h) -> p t h", t=4)
            CL = smallb[:, 256:384].rearrange("p (t h) -> p t h", t=4)
            NHh = smallb[:, 384:512].rearrange("p (t h) -> p t h", t=4)
            NM = smallb[:, 512:640].rearrange("p (t h) -> p t h", t=4)
            NL = smallb[:, 640:768].rearrange("p (t h) -> p t h", t=4)
            bsm = b_small
            fv, fb = W.get(win, 0, 16, 4 * D, NH, cin)
            for tb in range(4):
                bk = nbank()
                fns = [lambda e, k=k, tb=tb, bk=bk: e.matmul(
                    psum[bk][:, 0:NH], lhsT=hn[:, k, tb * 128:(tb + 1) * 128], rhs=fv[:, k, :],
                    start=(k == 0), stop=(k == NCH - 1)) for k in range(NCH)]
                P.pe_group(fns, reads=[fb] + b_hn, writes=[b_ps[bk]])
                P.op("dve", lambda e, tb=tb, bk=bk: e.tensor_tensor(
                    out=zf[:, tb, :], in0=psum[bk][:, 0:NH], in1=cst_sb[:, C_BF + NH * j:C_BF + NH * (j + 1)],
                    op=ALU.add), reads=[b_ps[bk], b_cst], writes=[bsm])
            P.op("act", lambda e: e.activation(out=small[:, 0:128], in_=small[:, 0:128], func=AF.Exp, scale=-1.0),
                 reads=[bsm], writes=[bsm])
            P.op("act", lambda e: e.activation(out=small[:, 128:256], in_=small[:, 0:128], func=AF.Ln, bias=1.0),
                 reads=[bsm], writes=[bsm])
            for tb in range(4):
                bk = nbank()
                fns = [lambda e, t2_=t2_, bk=bk, tb=tb: e.matmul(
                    psum[bk][:, 0:NH], lhsT=(negones[:] if t2_ < tb else cst_sb[:, C_NTRI:C_NTRI + 128]),
                    rhs=sp_[:, t2_, :], start=(t2_ == 0), stop=(t2_ == tb)) for t2_ in range(tb + 1)]
                P.pe_group(fns, reads=[bsm, b_c2, b_cst], writes=[b_ps[bk]])
                P.op("dve", lambda e, tb=tb, bk=bk: e.tensor_copy(out=lc[:, tb, :], in_=psum[bk][:, 0:NH]),
                     reads=[b_ps[bk]], writes=[bsm])
            bk = nbank()
            fns = [lambda e, t2_=t2_, bk=bk: e.matmul(psum[bk][:, 0:NH], lhsT=negones[:], rhs=sp_[:, t2_, :],
                                                       start=(t2_ == 0), stop=(t2_ == 3)) for t2_ in range(4)]
            P.pe_group(fns, reads=[bsm, b_c2], writes=[b_ps[bk]])
            P.op("dve", lambda e, bk=bk: e.tensor_copy(out=TOT[L][:, T, :], in_=psum[bk][:, 0:NH]),
                 reads=[b_ps[bk]], writes=[b_TOT[L]])
            lcf = small[:, 256:384]
            r1f = small[:, 384:512]
            P.op("dve", lambda e: e.tensor_copy(out=smallb[:, 0:128], in_=lcf), reads=[bsm], writes=[bsm])
            P.op("dve", lambda e: e.tensor_tensor(out=r1f, in0=lcf, in1=smallb[:, 0:128], op=ALU.subtract),
                 reads=[bsm], writes=[bsm])
            P.op("dve", lambda e: e.tensor_copy(out=smallb[:, 128:256], in_=r1f), reads=[bsm], writes=[bsm])
            P.op("dve", lambda e: e.tensor_tensor(out=r1f, in0=r1f, in1=smallb[:, 128:256], op=ALU.subtract),
                 reads=[bsm], writes=[bsm])
            P.op("dve", lambda e: e.tensor_copy(out=smallb[:, 256:384], in_=r1f), reads=[bsm], writes=[bsm])
            P.op("dve", lambda e: e.tensor_scalar(out=smallb[:, 384:768], in0=smallb[:, 0:384], scalar1=-1.0,
                                                  scalar2=None, op0=ALU.mult), reads=[bsm], writes=[bsm])
            P.op("dve", lambda e: e.memset(DTAB[:, T, :], 0.0), writes=[b_DT])
            for i in range(T - 1, -1, -1):
                P.op("dve", lambda e, i=i: e.tensor_tensor(out=DTAB[:, i, :], in0=DTAB[:, i + 1, :],
                                                           in1=TOT[L][:, i, :], op=ALU.add),
                     reads=[b_DT, b_TOT[L]], writes=[b_DT])

            def qk_slab(which, sq_i):
                col0 = (0 if which == "q" else D) + sq_i * SLABW
                wv, wbuf = W.get(win, 0, 16, col0, SLABW, cin)
                aug, baug = (augq, b_augq) if which == "q" else (augk, b_augk)
                gain = qg_s[:, 64 * j:64 * (j + 1)] if which == "q" else cst_sb[:, C_KG + 64 * j:C_KG + 64 * (j + 1)]
                h0 = 4 * sq_i
                cs = (CH, CM, CL) if which == "q" else (NHh, NM, NL)
                cbase = 64 if which == "q" else 67
                for n_, csrc in enumerate(cs):
                    P.op("pool", lambda e, csrc=csrc, n_=n_, aug=aug: e.tensor_copy(
                        out=aug[:, :, :, cbase + n_:cbase + n_ + 1], in_=csrc[:, :, h0:h0 + 4].unsqueeze(3)),
                        reads=[bsm], writes=[baug])
                for tb in range(4):
                    bk = nbank()
                    fns = [lambda e, k=k, tb=tb, bk=bk: e.matmul(
                        psum[bk][:, 0:SLABW], lhsT=hn[:, k, tb * 128:(tb + 1) * 128], rhs=wv[:, k, :],
                        start=(k == 0), stop=(k == NCH - 1)) for k in range(NCH)]
                    P.pe_group(fns, reads=[wbuf] + b_hn, writes=[b_ps[bk]])
                    P.op("act", lambda e, bk=bk: e.activation(out=sq, in_=psum[bk][:, 0:SLABW], func=AF.Square),
                         reads=[b_ps[bk]], writes=[bsm])
                    P.op("dve", lambda e: e.tensor_reduce(out=ss4, in_=sq.rearrange("p (h d) -> p h d", h=4),
                                                          axis=AX.X, op=ALU.add), reads=[bsm], writes=[bsm])
                    P.op("act", lambda e: e.activation(out=ss4, in_=ss4, func=AF.Sqrt, bias=EPS, scale=1.0 / HD),
                         reads=[bsm], writes=[bsm])
                    P.op("dve", lambda e: e.reciprocal(out=ss4, in_=ss4), reads=[bsm], writes=[bsm])
                    P.op("dve", lambda e, bk=bk: e.tensor_tensor(
                        out=t2, in0=psum[bk][:, 0:SLABW].rearrange("p (h d) -> p h d", h=4),
                        in1=ss4.unsqueeze(2).broadcast_to([128, 4, HD]), op=ALU.mult),
                        reads=[b_ps[bk], bsm], writes=[bsm])
                    P.op("dve", lambda e, tb=tb, aug=aug: e.tensor_tensor(
                        out=aug[:, tb, :, 0:HD], in0=t2, in1=gain.unsqueeze(1).broadcast_to([128, 4, HD]),
                        op=ALU.mult), reads=[bsm, b_c2, b_cst], writes=[baug])
                if which == "k":
                    ksi = rr["ks"]
                    rr["ks"] ^= 1
                    dstT, bdst = KS[ksi][:].rearrange("p (h t) -> p h t", t=TT), b_KS[ksi]
                for hh in range(4):
                    bk = nbank()
                    pbf = psum[bk][:].bitcast(BF16)
                    fns = [lambda e, tb=tb, hh=hh, pbf=pbf, aug=aug: e.transpose(
                        out=pbf[0:72, tb * 128:(tb + 1) * 128], in_=aug[:, tb, hh, :], identity=ident_bf[:])
                        for tb in range(4)]
                    P.pe_group(fns, reads=[baug, b_c2], writes=[b_ps[bk]])
                    if which == "q":
                        o_ap, ob = QT[0:72, h0 + hh, :], bQT
                    else:
                        o_ap, ob = dstT[0:72, hh, :], bdst
                    eng = alt()
                    if eng == "act":
                        P.op("act", lambda e, o_ap=o_ap, pbf=pbf: e.copy(out=o_ap, in_=pbf[0:72, 0:TT]),
                             reads=[b_ps[bk]], writes=[ob])
                    else:
                        P.op("dve", lambda e, o_ap=o_ap, pbf=pbf: e.tensor_copy(out=o_ap, in_=pbf[0:72, 0:TT]),
                             reads=[b_ps[bk]], writes=[ob])
                if which == "k":
                    return P.op("sp", lambda e, ksi=ksi: e.dma_start(out=kc[L][T, sq_i, :, :], in_=KS[ksi][0:72, :]),
                                reads=[bdst], lane=f"ks{ksi}")
                return None

            for s_ in range(8):
                qk_slab("q", s_)
            kst_tok = [qk_slab("k", s_) for s_ in range(8)]
            vst_tok = []
            for s_ in range(8):
                wv, wbuf = W.get(win, 0, 16, 2 * D + s_ * SLABW, SLABW, cin)
                vsi = rr["vs"]
                rr["vs"] ^= 1
                vv = VS[vsi][:].rearrange("p (a b c) -> p a b c", a=4, b=2)
                for tb in range(4):
                    bk = nbank()
                    fns = [lambda e, k=k, tb=tb, bk=bk, wv=wv: e.matmul(
                        psum[bk][:, 0:SLABW], lhsT=hn[:, k, tb * 128:(tb + 1) * 128], rhs=wv[:, k, :],
                        start=(k == 0), stop=(k == NCH - 1)) for k in range(NCH)]
                    P.pe_group(fns, reads=[wbuf] + b_hn, writes=[b_ps[bk]])
                    pv = psum[bk][:, 0:SLABW].rearrange("p (a b d) -> p a b d", a=2, b=2)
                    for par in range(2):
                        eng = alt()
                        o_ap = vv[:, tb, :, 128 * par:128 * par + 64]
                        i_ap = pv[:, :, par, :]
                        if eng == "act":
                            P.op("act", lambda e, o_ap=o_ap, i_ap=i_ap: e.copy(out=o_ap, in_=i_ap),
                                 reads=[b_ps[bk]], writes=[b_VS[vsi]])
                        else:
                            P.op("dve", lambda e, o_ap=o_ap, i_ap=i_ap: e.tensor_copy(out=o_ap, in_=i_ap),
                                 reads=[b_ps[bk]], writes=[b_VS[vsi]])
                vst_tok.append(P.op("sp", lambda e, vsi=vsi, s_=s_: e.dma_start(out=vc[L][T, s_, :, :], in_=VS[vsi][:]),
                                    reads=[b_VS[vsi]], lane=f"vs{vsi}"))
            for s_ in range(8):
                wv, wbuf = W.get(win, 0, 16, 3 * D + s_ * SLABW, SLABW, cin)
                for jj in range(2):
                    bk = nbank()
                    fns = [lambda e, k=k, jj=jj, bk=bk, wv=wv: e.matmul(
                        psum[bk][:], lhsT=wv[:, k, jj * 128:(jj + 1) * 128], rhs=hn[:, k, :],
                        start=(k == 0), stop=(k == NCH - 1)) for k in range(NCH)]
                    P.pe_group(fns, reads=[wbuf] + b_hn, writes=[b_ps[bk]])
                    P.op("act", lambda e, bk=bk, ch=2 * s_ + jj: e.activation(out=OG[:, ch, :], in_=psum[bk][:],
                                                                             func=AF.Sigmoid),
                         reads=[b_ps[bk]], writes=[b_OG[2 * s_ + jj]])

            kreqs = []
            vreqs = []
            for hg in range(8):
                for i in range(T + 1):
                    kd = [kst_tok[hg]] if i == T else []
                    vd = [vst_tok[hg]] if i == T else []
                    kreqs.append((lambda t_: t_[0:72, :], kc[L][i, hg, :, :], kd))
                    vreqs.append((lambda t_: t_[:], vc[L][i, hg, :, :], vd))
            if not P.dry:
                KSt.begin(kreqs)
                VSt.begin(vreqs)
                for r_ in range(2):
                    KSt.issue(r_)
                    VSt.issue(r_)
            OB = [0, 1, 2, 3]
            SB = [4, 5, 6]
            BC = 7
            for hg in range(8):
                steps = []
                for i in range(T + 1):
                    for kb in range(4):
                        for hh in range(4):
                            steps.append((i, kb, hh))
                kv = {}
                if not P.dry:
                    for i in range(T + 1):
                        kt, kbuf = KSt.slot(hg * (T + 1) + i)
                        vt, vbuf = VSt.slot(hg * (T + 1) + i)
                        kv[i] = (kt[:].rearrange("p (h t) -> p h t", t=TT), kbuf,
                                 vt[:].rearrange("p (a b c) -> p a b c", a=4, b=2), vbuf)
                else:
                    continue

                def emit_S(n):
                    i, kb, hh = steps[n]
                    h = 4 * hg + hh
                    kt, kbuf, _, _ = kv[i]
                    qlo = kb * 128 if i == T else 0
                    nco = TT - qlo
                    sbk = SB[n % 3]
                    fns = [lambda e: e.matmul(psum[sbk][:, 0:nco], lhsT=kt[0:70, hh, kb * 128:(kb + 1) * 128],
                                              rhs=QT[0:70, h, qlo:TT], start=True, stop=(i != T))]
                    if i == T:
                        fns.append(lambda e: e.matmul(psum[sbk][:, 0:128], lhsT=ident_bf[:], rhs=mask_bf[:],
                                                      start=False, stop=True))
                    P.pe_group(fns, reads=[kbuf, bQT, b_c2], writes=[b_ps[sbk]])

                def emit_E(n):
                    i, kb, hh = steps[n]
                    h = 4 * hg + hh
                    qlo = kb * 128 if i == T else 0
                    nco = TT - qlo
                    sbk = SB[n % 3]
                    pi = n % 3
                    P.op("act", lambda e: e.activation(out=PT[pi][:, 0:nco], in_=psum[sbk][:, 0:nco], func=AF.Exp,
                                                       bias=DTAB[:, i, h:h + 1], scale=1.0),
                         reads=[b_ps[sbk], b_DT], writes=[b_PT[pi]])

                def emit_PV(n):
                    i, kb, hh = steps[n]
                    _, _, vt, vbuf = kv[i]
                    qlo = kb * 128 if i == T else 0
                    nco = TT - qlo
                    pi = n % 3
                    obk = OB[hh]
                    pr = hh // 2
                    first = (i == 0 and kb == 0)
                    last = (i == T and kb == 3)
                    if hh % 2 == 0:
                        fn = lambda e: e.matmul(psum[obk][0:65, qlo:TT], lhsT=vt[:, kb, pr, 0:65],
                                                rhs=PT[pi][:, 0:nco], start=first, stop=last)
                    else:
                        fn = lambda e: e.matmul(psum[obk][:, qlo:TT], lhsT=vt[:, kb, pr, 64:192],
                                                rhs=PT[pi][:, 0:nco], start=first, stop=last)
                    P.pe_group([fn], reads=[vbuf, b_PT[pi]], writes=[b_ps[obk]])

                ns = len(steps)
                emit_S(0)
                if ns > 1:
                    emit_S(1)
                for n in range(ns):
                    emit_E(n)
                    if n + 2 < ns:
                        emit_S(n + 2)
                    emit_PV(n)
                    if n == ns - 1 or steps[n + 1][0] != steps[n][0]:
                        r_ = hg * (T + 1) + steps[n][0]
                        KSt.issue(r_ + 2)
                        VSt.issue(r_ + 2)
                for pr in range(2):
                    obe, obo = OB[2 * pr], OB[2 * pr + 1]
                    ch = 2 * hg + pr
                    P.op("act", lambda e, obe=obe: e.copy(out=RR[64:65, :], in_=psum[obe][64:65, :]),
                         reads=[b_ps[obe]], writes=[b_RR])
                    P.op("act", lambda e, obo=obo: e.copy(out=RR[0:1, :], in_=psum[obo][0:1, :]),
                         reads=[b_ps[obo]], writes=[b_RR])
                    P.pe_group([lambda e: e.matmul(psum[BC][:], lhsT=cst_sb[:, C_SEL:C_SEL + 128], rhs=RR[:],
                                                   start=True, stop=True)],
                               reads=[b_RR, b_cst], writes=[b_ps[BC]])
                    P.op("dve", lambda e: e.reciprocal(out=RC[:], in_=psum[BC][:]), reads=[b_ps[BC]], writes=[b_RC])
                    P.op("dve", lambda e, ch=ch: e.tensor_tensor(out=RC[:], in0=RC[:], in1=OG[:, ch, :], op=ALU.mult),
                         reads=[b_RC, b_OG[ch]], writes=[b_RC])
                    P.op("dve", lambda e, ch=ch, obe=obe: e.tensor_tensor(
                        out=OG[0:64, ch, :], in0=psum[obe][0:64, :], in1=RC[0:64, :], op=ALU.mult),
                        reads=[b_ps[obe], b_RC], writes=[b_OG[ch]])
                    P.op("dve", lambda e, ch=ch, obo=obo: e.tensor_tensor(
                        out=OG[64:128, ch, :], in0=psum[obo][64:128, :], in1=RC[64:128, :], op=ALU.mult),
                        reads=[b_ps[obo], b_RC], writes=[b_OG[ch]])
            rr["bank"] = 0
            for s_ in range(8):
                wv, wbuf = W.get(wout, 0, 16, s_ * SLABW, SLABW, cout)
                for jj in range(2):
                    bk = nbank()
                    fns = [lambda e, k=k, jj=jj, bk=bk, wv=wv: e.matmul(
                        psum[bk][:], lhsT=wv[:, k, jj * 128:(jj + 1) * 128], rhs=OG[:, k, :],
                        start=(k == 0), stop=(k == NCH - 1)) for k in range(NCH)]
                    P.pe_group(fns, reads=[wbuf] + b_OG, writes=[b_ps[bk]])
                    resid_add(2 * s_ + jj, bk)

        def store_out(T):
            if do_final:
                norm_stats()
                gcol = 128
            (b_os,) = reg1.switch(["ostage"])
            os_ = R1[:, 0:16384].bitcast(F32).rearrange("p (t d) -> p t d", t=4)
            fin = R1[:, 16384:16384 + 1024].bitcast(F32)
            (b_fin,) = [Buf("fin")]
            b_fin.r = list(b_os.r)
            for c in range(NCH):
                if do_final:
                    P.op("dve", lambda e, c=c: e.scalar_tensor_tensor(
                        out=fin, in0=hT[:, c, :], scalar=G(gcol + c), in1=rstd[:], op0=ALU.mult, op1=ALU.mult),
                        reads=[b_hT[c], b_rstd, b_cst], writes=[b_fin])
                    src_ap, src_b = fin, b_fin
                else:
                    src_ap, src_b = hT[:, c, :], b_hT[c]
                bk = nbank()
                fns = [lambda e, tb=tb, bk=bk, src_ap=src_ap: e.transpose(
                    out=psum[bk][:, tb * 128:(tb + 1) * 128], in_=src_ap[:, tb * 128:(tb + 1) * 128],
                    identity=cst_sb[:, C_ID:C_ID + 128]) for tb in range(4)]
                P.pe_group(fns, reads=[src_b, b_cst], writes=[b_ps[bk]])
                eng = alt()
                o_ap = os_[:, :, c * 128:(c + 1) * 128]
                i_ap = psum[bk][:].rearrange("p (t d) -> p t d", t=4)
                if eng == "act":
                    P.op("act", lambda e, o_ap=o_ap, i_ap=i_ap: e.copy(out=o_ap, in_=i_ap),
                         reads=[b_ps[bk]], writes=[b_os])
                else:
                    P.op("dve", lambda e, o_ap=o_ap, i_ap=i_ap: e.tensor_copy(out=o_ap, in_=i_ap),
                         reads=[b_ps[bk]], writes=[b_os])
            dst = out[T * TT:(T + 1) * TT, :].rearrange("(t p) d -> p t d", p=128)
            return P.op("sp", lambda e: e.dma_start(out=dst, in_=os_), reads=[b_os], lane="o")

        def walk():
            last = None
            for T in range(NT):
                load_x(T)
                for L in layers:
                    if L % 2 == 0:
                        pool_layer(L, T)
                    else:
                        fox_layer(L, T)
                    ffn_layer(L)
                last = store_out(T)
            return last

        emit_init()
        P.dry = True
        walk()
        P.dry = False
        W.pos = 0
        for k_ in rr:
            rr[k_] = 0
        reg1.live = []
        last = walk()
        P.lists["sp"].append(([(last.sem, last.val)], None, None, 0))

        with nc.Block() as block:
            def runner(eng):
                def f(e):
                    for waits, fn, tok, inc in P.lists[eng]:
                        for s_, v_ in waits:
                            e.wait_ge(s_, v_)
                        if fn is None:
                            continue
                        ins = fn(e)
                        if tok is not None:
                            ins.then_inc(tok.sem, inc)
                return f

            block.tensor(runner("pe"))
            block.scalar(runner("act"))
            block.vector(runner("dve"))
            block.gpsimd(runner("pool"))
            block.sync(runner("sp"))
    return nc


def make_consts(inputs):
    c = np.zeros((128, CW), np.float32)
    c[:, C_ID:C_ID + 128] = np.eye(128, dtype=np.float32)
    r = np.arange(128)
    c[:, C_NTRI:C_NTRI + 128] = -(r[:, None] <= r[None, :]).astype(np.float32)
    c[:, C_MASK:C_MASK + 128] = np.where(r[:, None] > r[None, :], MASKV, 0.0).astype(np.float32)
    c[64, C_SEL:C_SEL + 64] = 1.0
    c[0, C_SEL + 64:C_SEL + 128] = 1.0

    def fm(v):
        return np.asarray(v, np.float32).reshape(16, 128).T

    for L in range(4):
        c[:, C_G + 16 * L:C_G + 16 * (L + 1)] = fm(inputs["attn_norm_g"][L])
        c[:, C_G + 64 + 16 * L:C_G + 64 + 16 * (L + 1)] = fm(inputs["ffn_norm_g"][L])
    c[:, C_G + 128:C_G + 144] = fm(inputs["final_norm_g"])
    for j in range(2):
        c[:, C_G + 144 + 16 * j:C_G + 160 + 16 * j] = fm(inputs["pool_scale"][j])
        c[:, C_BF + 32 * j:C_BF + 32 * (j + 1)] = np.asarray(inputs["fox_b_f"][j], np.float32)[None, :]
        c[:, C_QG + 64 * j:C_QG + 64 * (j + 1)] = np.asarray(inputs["fox_q_norm_g"][j], np.float32)[None, :]
        c[:, C_KG + 64 * j:C_KG + 64 * (j + 1)] = np.asarray(inputs["fox_k_norm_g"][j], np.float32)[None, :]
    return c


def weight_map(inputs, layers):
    m = {}
    for L in layers:
        j = L // 2
        if L % 2 == 0:
            m[f"pw{j}"] = np.ascontiguousarray(np.asarray(inputs["pool_w"][j], np.float32).reshape(2048, 512))
        else:
            m[f"win{j}"] = np.ascontiguousarray(np.asarray(inputs["fox_w_in"][j], np.float32))
            m[f"wout{j}"] = np.ascontiguousarray(np.asarray(inputs["fox_w_out"][j], np.float32))
        m[f"wgu{L}"] = np.ascontiguousarray(np.asarray(inputs["ffn_w_gate_up"][L], np.float32))
        m[f"wdn{L}"] = np.ascontiguousarray(np.asarray(inputs["ffn_w_down"][L], np.float32))
    return m


def run_layers(xs, inputs, layers, do_final, NT):
    nc = build(NT, layers, do_final)
    cst = make_consts(inputs)
    wm = weight_map(inputs, layers)
    in_maps = []
    for xb in xs:
        d = {"x": np.ascontiguousarray(xb, dtype=np.float32), "cst": cst}
        d.update(wm)
        in_maps.append(d)
    res = run_bass_kernel_spmd(nc, in_maps, core_ids=list(range(len(xs))))
    return [np.asarray(r["out"]) for r in res.results]


FUSED = True


def kernel(**inputs):
    x = np.asarray(inputs["x"], np.float32)
    B, S, _ = x.shape
    NT = S // TT
    xs = [x[b] for b in range(B)]
    if FUSED:
        outs = run_layers(xs, inputs, [0, 1, 2, 3], True, NT)
    else:
        for L in range(4):
            xs = run_layers(xs, inputs, [L], L == 3, NT)
        outs = xs
    return np.stack(outs, axis=0).astype(np.float32)
```

```python
from contextlib import ExitStack

import numpy as np
import concourse.bass as bass
import concourse.mybir as mybir
from concourse.bass_utils import run_bass_kernel_spmd

F32 = mybir.dt.float32
BF16 = mybir.dt.bfloat16
AF = mybir.ActivationFunctionType
ALU = mybir.AluOpType
AX = mybir.AxisListType

D = 2048
NCH = 16
TT = 512
FF = 5632
NFC = 44
NH = 32
HD = 64
FOX_IN = 4 * D + NH
EPS = 1e-6
WINS = (2, 4, 8, 16)
SLABW = 256
SLOT_COLS = 22 * SLABW
NSLOT = 3
MASKV = -30000.0
SEM_MAX = 4000

C_ID = 0
C_NTRI = 128
C_MASK = 256
C_SEL = 384
C_G = 512
C_BF = C_G + 176
C_QG = C_BF + 64
C_KG = C_QG + 128
CW = C_KG + 128


class Tok:
    __slots__ = ("eng", "sem", "val")

    def __init__(self, eng, sem, val):
        self.eng, self.sem, self.val = eng, sem, val


class Buf:
    def __init__(self, name, const=False):
        self.name = name
        self.w = None
        self.r = []
        self.const = const


class Prog:
    ENGS = ("pe", "act", "dve", "pool", "sp")

    def __init__(self, nc, stack):
        self.nc = nc
        self.stack = stack
        self.lists = {e: [] for e in self.ENGS}
        self.cur = {}
        self.lanes = {}
        self.waited = {e: {} for e in self.ENGS}
        self.nsem = 0
        self.dry = False

    def newsem(self):
        s = self.stack.enter_context(self.nc.semaphore(f"sm{self.nsem}"))
        self.nsem += 1
        return s

    def _tok(self, eng):
        c = self.cur.get(eng)
        if c is None or c[1] >= SEM_MAX:
            c = [self.newsem(), 0]
            self.cur[eng] = c
        c[1] += 1
        return Tok(eng, c[0], c[1])

    def _lane(self, eng, lane):
        c = self.lanes.get(lane)
        if c is None or c[1] >= SEM_MAX:
            c = [self.newsem(), 0]
            self.lanes[lane] = c
        c[1] += 16
        return Tok("dma", c[0], c[1])

    def _waits(self, eng, reads, writes, deps):
        d = [t for t in deps if t is not None]
        for b in reads:
            if b.w is not None:
                d.append(b.w)
        for b in writes:
            if b.w is not None:
                d.append(b.w)
            d.extend(t for t in b.r if t is not None)
        best = {}
        for t in d:
            if eng == "pe" and t.eng == "pe":
                continue
            k = id(t.sem)
            if self.waited[eng].get(k, -1) >= t.val:
                continue
            if k not in best or best[k][1] < t.val:
                best[k] = (t.sem, t.val)
        for k, (s, v) in best.items():
            self.waited[eng][k] = v
        return list(best.values())

    def _note(self, tok, reads, writes):
        for b in reads:
            if not b.const:
                b.r.append(tok)
        for b in writes:
            b.w = tok
            b.r = []

    def op(self, eng, fn, reads=(), writes=(), deps=(), lane=None):
        if self.dry:
            return None
        waits = self._waits(eng, reads, writes, deps)
        if lane is not None:
            tok, inc = self._lane(eng, lane), 16
        else:
            tok, inc = self._tok(eng), 1
        self.lists[eng].append((waits, fn, tok, inc))
        self._note(tok, reads, writes)
        return tok

    def pe_group(self, fns, reads=(), writes=(), deps=()):
        if self.dry:
            return None
        waits = self._waits("pe", reads, writes, deps)
        tok = self._tok("pe")
        n = len(fns)
        for i, fn in enumerate(fns):
            self.lists["pe"].append((waits if i == 0 else [], fn, tok if i == n - 1 else None, 1))
        self._note(tok, reads, writes)
        return tok

    def run(self, eng, e):
        for waits, fn, tok, inc in self.lists[eng]:
            for s, v in waits:
                e.wait_ge(s, v)
            ins = fn(e)
            if tok is not None:
                ins.then_inc(tok.sem, inc)


def fence_of(bufs):
    toks = []
    for b in bufs:
        if b.w is not None:
            toks.append(b.w)
        toks.extend(t for t in b.r if t is not None)
    best = {}
    for t in toks:
        k = id(t.sem)
        if k not in best or best[k].val < t.val:
            best[k] = t
    return list(best.values())


class Region:
    def __init__(self):
        self.live = []

    def switch(self, names):
        f = fence_of(self.live)
        bufs = []
        for n in names:
            b = Buf(n)
            b.r = list(f)
            bufs.append(b)
        self.live = bufs
        return bufs


class WStream:
    def __init__(self, P, slots):
        self.P = P
        self.slots = slots
        self.reqs = []
        self.pos = 0
        self.issued = 0
        self.hook = None
        self.conv_of = None

    def _view(self, slot_t, nk, ncols):
        return slot_t[:, 0:nk * ncols].rearrange("p (k c) -> p k c", c=ncols)

    def get(self, src2d, r0, nk, c0, ncols, conv):
        i = self.pos
        self.pos += 1
        slot_t, slot_b = self.slots[i % len(self.slots)]
        view = self._view(slot_t, nk, ncols)
        if self.P.dry:
            self.reqs.append((src2d, r0, nk, c0, ncols, conv))
            return view, slot_b
        if self.hook is not None:
            self.hook()
        while self.issued < min(len(self.reqs), i + len(self.slots)):
            self._issue(self.issued)
            self.issued += 1
        return view, slot_b

    def _issue(self, i):
        src2d, r0, nk, c0, ncols, conv = self.reqs[i]
        conv = self.conv_of(conv)
        slot_t, slot_b = self.slots[i % len(self.slots)]
        dst = self._view(slot_t, nk, ncols)
        src = src2d[r0:r0 + nk * 128, c0:c0 + ncols].rearrange("(k p) c -> p k c", p=128)
        self.P.op("sp", lambda e, dst=dst, src=src: e.dma_start(out=dst, in_=src),
                  writes=[slot_b], deps=[conv], lane=f"w{i % len(self.slots)}")


class KVStream:
    def __init__(self, P, name, slots):
        self.P = P
        self.name = name
        self.slots = slots
        self.n = 0
        self.reqs = []
        self.base = 0

    def begin(self, reqs):
        self.base = self.n
        self.reqs = reqs
        self.n += len(reqs)

    def issue(self, k):
        if k >= len(self.reqs):
            return
        dst_fn, src, deps = self.reqs[k]
        slot_t, slot_b = self.slots[(self.base + k) % len(self.slots)]
        dst = dst_fn(slot_t)
        self.P.op("sp", lambda e, dst=dst, src=src: e.dma_start(out=dst, in_=src),
                  writes=[slot_b], deps=deps, lane=f"{self.name}{(self.base + k) % len(self.slots)}")

    def slot(self, k):
        return self.slots[(self.base + k) % len(self.slots)]


def build(NT, layers, do_final):
    nc = bass.Bass("TRN2", target_bir_lowering=False)
    S = NT * TT
    x = nc.dram_tensor("x", [S, D], F32, kind="ExternalInput").ap()
    out = nc.dram_tensor("out", [S, D], F32, kind="ExternalOutput").ap()
    cst = nc.dram_tensor("cst", [128, CW], F32, kind="ExternalInput").ap()

    wsrc = {}

    def decl_w(name, rows, cols):
        a = nc.dram_tensor(name, [rows, cols], F32, kind="ExternalInput").ap()
        b = nc.dram_tensor(name + "_bf", [rows, cols], BF16, kind="Internal").ap()
        wsrc[name] = (a, b, rows, cols)

    worder = []
    for L in layers:
        j = L // 2
        if L % 2 == 0:
            decl_w(f"pw{j}", 2048, 512)
            worder.append(f"pw{j}")
        else:
            decl_w(f"win{j}", D, FOX_IN)
            decl_w(f"wout{j}", D, D)
            worder += [f"win{j}", f"wout{j}"]
        decl_w(f"wgu{L}", D, 2 * FF)
        decl_w(f"wdn{L}", FF, D)
        worder += [f"wgu{L}", f"wdn{L}"]
    fox_layers = [L for L in layers if L % 2 == 1]
    kc = {}
    vc = {}
    for L in fox_layers:
        kc[L] = nc.dram_tensor(f"kc{L}", [NT, 8, 72, 4 * TT], BF16, kind="Internal").ap()
        vc[L] = nc.dram_tensor(f"vc{L}", [NT, 8, 128, 4 * 2 * 192], BF16, kind="Internal").ap()

    with ExitStack() as st:
        def sb(name, shape, dt):
            return st.enter_context(nc.sbuf_tensor(name, shape, dt))

        P = Prog(nc, st)
        cst_sb = sb("cst_sb", [128, CW], F32)
        hT = sb("hT", [128, NCH, TT], F32)
        hn = sb("hn", [128, NCH, TT], BF16)
        R1 = sb("R1", [128, NFC * TT], BF16)
        OG = sb("OG", [128, NCH, TT], BF16)
        KS = [sb(f"KS{i}", [128, 4 * TT], BF16) for i in range(2)]
        KLs = [sb(f"KL{i}", [128, 4 * TT], BF16) for i in range(2)]
        VS = [sb(f"VS{i}", [128, 4 * 2 * 192], BF16) for i in range(2)]
        VLs = [sb(f"VL{i}", [128, 4 * 2 * 192], BF16) for i in range(2)]
        slots = [sb(f"slab{i}", [128, SLOT_COLS], BF16) for i in range(NSLOT)]
        PT = [sb(f"PT{i}", [128, TT], BF16) for i in range(3)]
        tmpf = [sb(f"tmpf{i}", [128, TT], F32) for i in range(2)]
        rstd = sb("rstd", [128, TT], F32)
        RR = sb("RR", [128, TT], F32)
        RC = sb("RC", [128, TT], F32)
        ident_bf = sb("ident_bf", [128, 128], BF16)
        ones_bf = sb("ones_bf", [128, 128], BF16)
        mask_bf = sb("mask_bf", [128, 128], BF16)
        negones = sb("negones", [128, 128], F32)
        halo = {L: sb(f"halo{L}", [128, NCH, 16], F32) for L in layers if L % 2 == 0}
        augq = sb("augq", [128, 4, 4, 72], BF16)
        augk = sb("augk", [128, 4, 4, 72], BF16)
        small = sb("small", [128, 1536], F32)
        smallb = sb("smallb", [128, 768], BF16)
        TOT = {L: sb(f"TOT{L}", [128, NT, NH], F32) for L in fox_layers}
        DTAB = sb("DTAB", [128, NT, NH], F32)
        qg_s = sb("qg_s", [128, 128], F32)
        psum = [st.enter_context(nc.psum_tensor(f"ps{i}", [128, TT], F32)) for i in range(8)]

        b_cst = Buf("cst", const=True)
        b_hT = [Buf(f"hT{c}") for c in range(NCH)]
        b_hn = [Buf(f"hn{c}") for c in range(NCH)]
        b_OG = [Buf(f"OG{c}") for c in range(NCH)]
        b_ps = [Buf(f"ps{i}") for i in range(8)]
        b_KS = [Buf(f"KS{i}") for i in range(2)]
        b_KL = [Buf(f"KL{i}") for i in range(2)]
        b_VS = [Buf(f"VS{i}") for i in range(2)]
        b_VL = [Buf(f"VL{i}") for i in range(2)]
        b_slots = [Buf(f"slab{i}") for i in range(NSLOT)]
        b_PT = [Buf(f"PT{i}") for i in range(3)]
        b_tmpf = [Buf(f"tmpf{i}") for i in range(2)]
        b_rstd = Buf("rstd")
        b_RR = Buf("RR")
        b_RC = Buf("RC")
        b_c2 = Buf("consts2", const=True)
        b_halo = {L: Buf(f"halo{L}") for L in halo}
        b_augq = Buf("augq")
        b_augk = Buf("augk")
        b_small = Buf("small")
        b_nrm = [Buf("nrmA"), Buf("nrmB")]
        b_TOT = {L: Buf(f"TOT{L}") for L in fox_layers}
        b_DT = Buf("DTAB")
        reg1 = Region()

        W = WStream(P, list(zip(slots, b_slots)))
        KSt = KVStream(P, "kl", list(zip(KLs, b_KL)))
        VSt = KVStream(P, "vl", list(zip(VLs, b_VL)))

        rr = {"bank": 0, "tmp": 0, "alt": 0, "ks": 0, "vs": 0, "pt": 0}

        def nbank():
            i = rr["bank"]
            rr["bank"] = (i + 1) % 8
            return i

        def ntmp():
            i = rr["tmp"]
            rr["tmp"] = (i + 1) % 2
            return i

        def alt():
            rr["alt"] ^= 1
            return "act" if rr["alt"] else "dve"

        G = lambda col: cst_sb[:, C_G + col:C_G + col + 1]

        conv_tok = {}
        pending = []

        def nchunks(name):
            return {"pw": 1, "wi": 16, "wo": 4, "wg": 16, "wd": 8}[name[:2]]

        def emit_chunk(name, idx):
            a, b, rows, cols = wsrc[name]
            n = nchunks(name)
            rs = rows // n
            tok = P.op("pool", lambda e: e.dma_start(out=b[idx * rs:(idx + 1) * rs, :],
                                                     in_=a[idx * rs:(idx + 1) * rs, :]), lane="cv_" + name)
            if idx == n - 1:
                conv_tok[name] = tok

        def flush(name):
            todo = [p_ for p_ in pending if p_[0] == name]
            for p_ in todo:
                pending.remove(p_)
                emit_chunk(*p_)

        def conv_hook():
            if pending:
                emit_chunk(*pending.pop(0))

        def conv_of(name):
            if name is None:
                return None
            if name not in conv_tok:
                flush(name)
            return conv_tok[name]

        W.hook = conv_hook
        W.conv_of = conv_of

        def emit_init():
            P.op("sp", lambda e: e.dma_start(out=cst_sb[:], in_=cst), writes=[b_cst], lane="cst")
            first = worder[:(3 if layers[0] % 2 == 0 else 4)]
            for name in worder:
                for idx in range(nchunks(name)):
                    pending.append((name, idx))
            for name in first:
                flush(name)
            P.op("dve", lambda e: e.tensor_copy(out=ident_bf[:], in_=cst_sb[:, C_ID:C_ID + 128]),
                 reads=[b_cst], writes=[b_c2])
            P.op("dve", lambda e: e.tensor_copy(out=mask_bf[:], in_=cst_sb[:, C_MASK:C_MASK + 128]),
                 reads=[b_cst], writes=[b_c2])
            P.op("dve", lambda e: e.memset(ones_bf[:], 1.0), writes=[b_c2])
            P.op("dve", lambda e: e.memset(negones[:], -1.0), writes=[b_c2])
            P.op("dve", lambda e: e.memset(RR[:], 0.0), writes=[b_RR])
            P.op("dve", lambda e: e.tensor_scalar(out=qg_s[:], in0=cst_sb[:, C_QG:C_QG + 128],
                                                  scalar1=float(HD ** -0.5), scalar2=None, op0=ALU.mult),
                 reads=[b_cst], writes=[b_c2])
            for L in halo:
                P.op("dve", lambda e, L=L: e.memset(halo[L][:], 0.0), writes=[b_halo[L]])
            for t_, bb in list(zip(VS, b_VS)) + list(zip(VLs, b_VL)):
                v = t_[:].rearrange("p (a b c) -> p a b c", a=4, b=2)
                P.op("dve", lambda e, v=v: e.memset(v[:, :, :, 64:128], 0.0), writes=[bb])
                P.op("dve", lambda e, v=v: e.memset(v[:, :, :, 64:65], 1.0), writes=[bb])
            P.op("dve", lambda e: e.memset(augq[:], 0.0), writes=[b_augq])
            P.op("dve", lambda e: e.memset(augq[:, :, :, 67:70], 1.0), writes=[b_augq])
            P.op("dve", lambda e: e.memset(augk[:], 0.0), writes=[b_augk])
            P.op("dve", lambda e: e.memset(augk[:, :, :, 64:67], 1.0), writes=[b_augk])

        def load_x(T):
            (b_xs,) = reg1.switch(["xstage"])
            xs = R1[:, 0:16384].bitcast(F32).rearrange("p (t d) -> p t d", t=4)
            src = x[T * TT:(T + 1) * TT, :].rearrange("(t p) d -> p t d", p=128)
            P.op("sp", lambda e: e.dma_start(out=xs, in_=src), writes=[b_xs], lane="x")
            for c in range(NCH):
                bk = nbank()
                fns = [lambda e, tb=tb, c=c, bk=bk: e.transpose(
                    out=psum[bk][:, tb * 128:(tb + 1) * 128], in_=xs[:, tb, c * 128:(c + 1) * 128],
                    identity=cst_sb[:, C_ID:C_ID + 128]) for tb in range(4)]
                P.pe_group(fns, reads=[b_xs, b_cst], writes=[b_ps[bk]])
                eng = alt()
                if eng == "act":
                    P.op("act", lambda e, c=c, bk=bk: e.copy(out=hT[:, c, :], in_=psum[bk][:]),
                         reads=[b_ps[bk]], writes=[b_hT[c]])
                else:
                    P.op("dve", lambda e, c=c, bk=bk: e.tensor_copy(out=hT[:, c, :], in_=psum[bk][:]),
                         reads=[b_ps[bk]], writes=[b_hT[c]])

        def norm_stats():
            for c in range(NCH):
                P.op("act", lambda e, c=c: e.activation(out=hn[:, c, :], in_=hT[:, c, :], func=AF.Square),
                     reads=[b_hT[c]], writes=[b_hn[c]])
            bk = nbank()
            fns = [lambda e, c=c, bk=bk: e.matmul(psum[bk][:], lhsT=ones_bf[:], rhs=hn[:, c, :],
                                                  start=(c == 0), stop=(c == NCH - 1)) for c in range(NCH)]
            P.pe_group(fns, reads=b_hn + [b_c2], writes=[b_ps[bk]])
            P.op("act", lambda e, bk=bk: e.activation(out=rstd[:], in_=psum[bk][:], func=AF.Sqrt,
                                                      bias=EPS, scale=1.0 / D),
                 reads=[b_ps[bk]], writes=[b_rstd])
            P.op("dve", lambda e: e.reciprocal(out=rstd[:], in_=rstd[:]), reads=[b_rstd], writes=[b_rstd])

        def normalize(gcol0):
            for c in range(NCH):
                P.op("dve", lambda e, c=c: e.scalar_tensor_tensor(
                    out=hn[:, c, :], in0=hT[:, c, :], scalar=G(gcol0 + c), in1=rstd[:],
                    op0=ALU.mult, op1=ALU.mult),
                    reads=[b_hT[c], b_rstd, b_cst], writes=[b_hn[c]])

        def resid_add(dc, bk, scale_col=None):
            if scale_col is None:
                P.op("dve", lambda e: e.tensor_tensor(out=hT[:, dc, :], in0=psum[bk][:], in1=hT[:, dc, :],
                                                      op=ALU.add),
                     reads=[b_ps[bk], b_hT[dc]], writes=[b_hT[dc]])
            else:
                P.op("dve", lambda e: e.scalar_tensor_tensor(
                    out=hT[:, dc, :], in0=psum[bk][:], scalar=G(scale_col), in1=hT[:, dc, :],
                    op0=ALU.mult, op1=ALU.add),
                    reads=[b_ps[bk], b_hT[dc], b_cst], writes=[b_hT[dc]])

        def pool_layer(L, T):
            j = L // 2
            norm_stats()
            bX, bY, bZ = reg1.switch(["pX", "pY", "pZ"])
            XW = 4 * 528
            views = [R1[:, i * 2 * XW:(i + 1) * 2 * XW].bitcast(F32).rearrange("p (k t) -> p k t", k=4)
                     for i in range(3)]
            X, Y, Z = views
            for g in range(4):
                win = WINS[g]
                c0 = 4 * g
                P.op("dve", lambda e, c0=c0: e.tensor_copy(out=X[:, :, 0:16], in_=halo[L][:, c0:c0 + 4, :]),
                     reads=[b_halo[L]], writes=[bX])
                for k in range(4):
                    c = c0 + k
                    P.op("dve", lambda e, k=k, c=c: e.scalar_tensor_tensor(
                        out=X[:, k, 16:528], in0=hT[:, c, :], scalar=G(16 * L + c), in1=rstd[:],
                        op0=ALU.mult, op1=ALU.mult),
                        reads=[b_hT[c], b_rstd, b_cst], writes=[bX])
                P.op("dve", lambda e, c0=c0: e.tensor_copy(out=halo[L][:, c0:c0 + 4, :], in_=X[:, :, 512:528]),
                     reads=[bX], writes=[b_halo[L]])
                src, bsrc = X, bX
                pp = [(Y, bY), (Z, bZ)]
                sh = 1
                lo = 1
                n = 0
                while sh < win:
                    dst, bdst = pp[n % 2]
                    P.op("dve", lambda e, dst=dst, src=src, lo=lo, sh=sh: e.tensor_tensor(
                        out=dst[:, :, lo:528], in0=src[:, :, lo:528], in1=src[:, :, lo - sh:528 - sh],
                        op=ALU.add), reads=[bsrc], writes=[bdst])
                    src, bsrc = dst, bdst
                    sh *= 2
                    lo = 2 * sh - 1
                    n += 1
                Sm, bS = src, bsrc
                P.op("dve", lambda e, Sm=Sm, c0=c0, win=win: e.scalar_tensor_tensor(
                    out=hn[:, c0:c0 + 4, :], in0=Sm[:, :, 16:528], scalar=1.0 / win, in1=X[:, :, 16:528],
                    op0=ALU.mult, op1=ALU.subtract),
                    reads=[bS, bX], writes=b_hn[c0:c0 + 4])
                if T == 0:
                    for t in range(win - 1):
                        P.op("dve", lambda e, Sm=Sm, c0=c0, t=t: e.scalar_tensor_tensor(
                            out=hn[:, c0:c0 + 4, t:t + 1], in0=Sm[:, :, 16 + t:17 + t], scalar=1.0 / (t + 1),
                            in1=X[:, :, 16 + t:17 + t], op0=ALU.mult, op1=ALU.subtract),
                            reads=[bS, bX], writes=b_hn[c0:c0 + 4])
            a, wb, _, _ = wsrc[f"pw{j}"]
            for half in range(2):
                wv, wbuf = W.get(wb, 0, 16, half * SLABW, SLABW, f"pw{j}")
                for gg in range(4):
                    for jj in range(2):
                        dc = 4 * gg + 2 * half + jj
                        bk = nbank()
                        fns = [lambda e, k=k, gg=gg, jj=jj, bk=bk, wv=wv: e.matmul(
                            psum[bk][:], lhsT=wv[:, 4 * gg + k, jj * 128:(jj + 1) * 128], rhs=hn[:, 4 * gg + k, :],
                            start=(k == 0), stop=(k == 3)) for k in range(4)]
                        P.pe_group(fns, reads=[wbuf] + b_hn[4 * gg:4 * gg + 4], writes=[b_ps[bk]])
                        resid_add(dc, bk, scale_col=144 + 16 * j + dc)

        def ffn_layer(L):
            norm_stats()
            normalize(64 + 16 * L)
            bact = reg1.switch([f"act{f}" for f in range(NFC)])
            act = R1[:].rearrange("p (f t) -> p f t", t=TT)
            _, wgu, _, _ = wsrc[f"wgu{L}"]
            _, wdn, _, _ = wsrc[f"wdn{L}"]
            cgu = f"wgu{L}"
            cdn = f"wdn{L}"
            for fp in range(22):
                bg = [nbank(), nbank()]
                bu = [nbank(), nbank()]
                for (col0, bks) in ((fp * SLABW, bg), (FF + fp * SLABW, bu)):
                    wv, wbuf = W.get(wgu, 0, 16, col0, SLABW, cgu)
                    fns = []
                    for jj in range(2):
                        fns += [lambda e, k=k, jj=jj, wv=wv, bk=bks[jj]: e.matmul(
                            psum[bk][:], lhsT=wv[:, k, jj * 128:(jj + 1) * 128], rhs=hn[:, k, :],
                            start=(k == 0), stop=(k == NCH - 1)) for k in range(NCH)]
                    P.pe_group(fns, reads=[wbuf] + b_hn, writes=[b_ps[bks[0]], b_ps[bks[1]]])
                for jj in range(2):
                    fc = 2 * fp + jj
                    ti = ntmp()
                    P.op("act", lambda e, ti=ti, bk=bg[jj]: e.activation(out=tmpf[ti][:], in_=psum[bk][:],
                                                                        func=AF.Silu),
                         reads=[b_ps[bg[jj]]], writes=[b_tmpf[ti]])
                    P.op("dve", lambda e, ti=ti, bk=bu[jj], fc=fc: e.tensor_tensor(
                        out=act[:, fc, :], in0=tmpf[ti][:], in1=psum[bk][:], op=ALU.mult),
                        reads=[b_tmpf[ti], b_ps[bu[jj]]], writes=[bact[fc]])
            for dp in range(8):
                bks = [nbank(), nbank()]
                for fh in range(2):
                    wv, wbuf = W.get(wdn, fh * 2816, 22, dp * SLABW, SLABW, cdn)
                    fns = []
                    for k in range(22):
                        for jj in range(2):
                            fns.append(lambda e, k=k, jj=jj, fh=fh, wv=wv, bk=bks[jj]: e.matmul(
                                psum[bk][:], lhsT=wv[:, k, jj * 128:(jj + 1) * 128], rhs=act[:, fh * 22 + k, :],
                                start=(fh == 0 and k == 0), stop=(fh == 1 and k == 21)))
                    P.pe_group(fns, reads=[wbuf] + bact[fh * 22:(fh + 1) * 22],
                               writes=[b_ps[bks[0]], b_ps[bks[1]]])
                for jj in range(2):
                    resid_add(2 * dp + jj, bks[jj])

        def fox_layer(L, T):
            j = L // 2
            norm_stats()
            normalize(16 * L)
            (bQT,) = reg1.switch(["QT"])
            QT = R1[:, 0:NH * TT].rearrange("p (h t) -> p h t", t=TT)
            _, win, _, _ = wsrc[f"win{j}"]
            _, wout, _, _ = wsrc[f"wout{j}"]
            cin = f"win{j}"
            cout = f"wout{j}"
            zf = small[:, 0:128].rearrange("p (t h) -> p t h", t=4)
            sp_ = small[:, 128:256].rearrange("p (t h) -> p t h", t=4)
            lc = small[:, 256:384].rearrange("p (t h) -> p t h", t=4)
            r1 = small[:, 384:512].rearrange("p (t h) -> p t h", t=4)
            nsets = [(small[:, 1024 + 4 * i_:1028 + 4 * i_], small[:, 512 + 256 * i_:768 + 256 * i_], b_nrm[i_])
                     for i_ in range(2)]
            CH = smallb[:, 0:128].rearrange("p (t h) -> p t h", t=4)
            CM = smallb[:, 128:256].rearrange("p (t h) -> p t h", t=4)
            CL = smallb[:, 256:384].rearrange("p (t h) -> p t h", t=4)
            NHh = smallb[:, 384:512].rearrange("p (t h) -> p t h", t=4)
            NM = smallb[:, 512:640].rearrange("p (t h) -> p t h", t=4)
            NL = smallb[:, 640:768].rearrange("p (t h) -> p t h", t=4)
            bsm = b_small
            fv, fb = W.get(win, 0, 16, 4 * D, NH, cin)
            for tb in range(4):
                bk = nbank()
                fns = [lambda e, k=k, tb=tb, bk=bk: e.matmul(
                    psum[bk][:, 0:NH], lhsT=hn[:, k, tb * 128:(tb + 1) * 128], rhs=fv[:, k, :],
                    start=(k == 0), stop=(k == NCH - 1)) for k in range(NCH)]
                P.pe_group(fns, reads=[fb] + b_hn, writes=[b_ps[bk]])
                P.op("dve", lambda e, tb=tb, bk=bk: e.tensor_tensor(
                    out=zf[:, tb, :], in0=psum[bk][:, 0:NH], in1=cst_sb[:, C_BF + NH * j:C_BF + NH * (j + 1)],
                    op=ALU.add), reads=[b_ps[bk], b_cst], writes=[bsm])
            P.op("act", lambda e: e.activation(out=small[:, 0:128], in_=small[:, 0:128], func=AF.Exp, scale=-1.0),
                 reads=[bsm], writes=[bsm])
            P.op("act", lambda e: e.activation(out=small[:, 128:256], in_=small[:, 0:128], func=AF.Ln, bias=1.0),
                 reads=[bsm], writes=[bsm])
            for tb in range(4):
                bk = nbank()
                fns = [lambda e, t2_=t2_, bk=bk, tb=tb: e.matmul(
                    psum[bk][:, 0:NH], lhsT=(negones[:] if t2_ < tb else cst_sb[:, C_NTRI:C_NTRI + 128]),
                    rhs=sp_[:, t2_, :], start=(t2_ == 0), stop=(t2_ == tb)) for t2_ in range(tb + 1)]
                P.pe_group(fns, reads=[bsm, b_c2, b_cst], writes=[b_ps[bk]])
                P.op("dve", lambda e, tb=tb, bk=bk: e.tensor_copy(out=lc[:, tb, :], in_=psum[bk][:, 0:NH]),
                     reads=[b_ps[bk]], writes=[bsm])
            bk = nbank()
            fns = [lambda e, t2_=t2_, bk=bk: e.matmul(psum[bk][:, 0:NH], lhsT=negones[:], rhs=sp_[:, t2_, :],
                                                       start=(t2_ == 0), stop=(t2_ == 3)) for t2_ in range(4)]
            P.pe_group(fns, reads=[bsm, b_c2], writes=[b_ps[bk]])
            P.op("dve", lambda e, bk=bk: e.tensor_copy(out=TOT[L][:, T, :], in_=psum[bk][:, 0:NH]),
                 reads=[b_ps[bk]], writes=[b_TOT[L]])
            lcf = small[:, 256:384]
            r1f = small[:, 384:512]
            P.op("dve", lambda e: e.tensor_copy(out=smallb[:, 0:128], in_=lcf), reads=[bsm], writes=[bsm])
            P.op("dve", lambda e: e.tensor_tensor(out=r1f, in0=lcf, in1=smallb[:, 0:128], op=ALU.subtract),
                 reads=[bsm], writes=[bsm])
            P.op("dve", lambda e: e.tensor_copy(out=smallb[:, 128:256], in_=r1f), reads=[bsm], writes=[bsm])
            P.op("dve", lambda e: e.tensor_tensor(out=r1f, in0=r1f, in1=smallb[:, 128:256], op=ALU.subtract),
                 reads=[bsm], writes=[bsm])
            P.op("dve", lambda e: e.tensor_copy(out=smallb[:, 256:384], in_=r1f), reads=[bsm], writes=[bsm])
            P.op("dve", lambda e: e.tensor_scalar(out=smallb[:, 384:768], in0=smallb[:, 0:384], scalar1=-1.0,
                                                  scalar2=None, op0=ALU.mult), reads=[bsm], writes=[bsm])
            P.op("dve", lambda e: e.memset(DTAB[:, T, :], 0.0), writes=[b_DT])
            for i in range(T - 1, -1, -1):
                P.op("dve", lambda e, i=i: e.tensor_tensor(out=DTAB[:, i, :], in0=DTAB[:, i + 1, :],
                                                           in1=TOT[L][:, i, :], op=ALU.add),
                     reads=[b_DT, b_TOT[L]], writes=[b_DT])

            def qk_slab(which, sq_i):
                col0 = (0 if which == "q" else D) + sq_i * SLABW
                wv, wbuf = W.get(win, 0, 16, col0, SLABW, cin)
                aug, baug = (augq, b_augq) if which == "q" else (augk, b_augk)
                gain = qg_s[:, 64 * j:64 * (j + 1)] if which == "q" else cst_sb[:, C_KG + 64 * j:C_KG + 64 * (j + 1)]
                h0 = 4 * sq_i
                cs = (CH, CM, CL) if which == "q" else (NHh, NM, NL)
                cbase = 64 if which == "q" else 67
                for n_, csrc in enumerate(cs):
                    P.op("pool", lambda e, csrc=csrc, n_=n_, aug=aug: e.tensor_copy(
                        out=aug[:, :, :, cbase + n_:cbase + n_ + 1], in_=csrc[:, :, h0:h0 + 4].unsqueeze(3)),
                        reads=[bsm], writes=[baug])
                for tb in range(4):
                    bk = nbank()
                    fns = [lambda e, k=k, tb=tb, bk=bk: e.matmul(
                        psum[bk][:, 0:SLABW], lhsT=hn[:, k, tb * 128:(tb + 1) * 128], rhs=wv[:, k, :],
                        start=(k == 0), stop=(k == NCH - 1)) for k in range(NCH)]
                    P.pe_group(fns, reads=[wbuf] + b_hn, writes=[b_ps[bk]])
                    ss4, sq, bn = nsets[tb % 2]
                    t2 = sq.rearrange("p (h d) -> p h d", h=4)
                    P.op("act", lambda e, bk=bk, sq=sq: e.activation(out=sq, in_=psum[bk][:, 0:SLABW], func=AF.Square),
                         reads=[b_ps[bk]], writes=[bn])
                    P.op("dve", lambda e, ss4=ss4, sq=sq: e.tensor_reduce(
                        out=ss4, in_=sq.rearrange("p (h d) -> p h d", h=4), axis=AX.X, op=ALU.add),
                        reads=[bn], writes=[bn])
                    P.op("act", lambda e, ss4=ss4: e.activation(out=ss4, in_=ss4, func=AF.Sqrt, bias=EPS,
                                                                scale=1.0 / HD), reads=[bn], writes=[bn])
                    P.op("dve", lambda e, ss4=ss4: e.reciprocal(out=ss4, in_=ss4), reads=[bn], writes=[bn])
                    P.op("dve", lambda e, bk=bk, ss4=ss4, t2=t2: e.tensor_tensor(
                        out=t2, in0=psum[bk][:, 0:SLABW].rearrange("p (h d) -> p h d", h=4),
                        in1=ss4.unsqueeze(2).broadcast_to([128, 4, HD]), op=ALU.mult),
                        reads=[b_ps[bk], bn], writes=[bn])
                    P.op("dve", lambda e, tb=tb, aug=aug, t2=t2: e.tensor_tensor(
                        out=aug[:, tb, :, 0:HD], in0=t2, in1=gain.unsqueeze(1).broadcast_to([128, 4, HD]),
                        op=ALU.mult), reads=[bn, b_c2, b_cst], writes=[baug])
                if which == "k":
                    ksi = rr["ks"]
                    rr["ks"] ^= 1
                    dstT, bdst = KS[ksi][:].rearrange("p (h t) -> p h t", t=TT), b_KS[ksi]
                for hh in range(4):
                    bk = nbank()
                    pbf = psum[bk][:].bitcast(BF16)
                    fns = [lambda e, tb=tb, hh=hh, pbf=pbf, aug=aug: e.transpose(
                        out=pbf[0:72, tb * 128:(tb + 1) * 128], in_=aug[:, tb, hh, :], identity=ident_bf[:])
                        for tb in range(4)]
                    P.pe_group(fns, reads=[baug, b_c2], writes=[b_ps[bk]])
                    if which == "q":
                        o_ap, ob = QT[0:72, h0 + hh, :], bQT
                    else:
                        o_ap, ob = dstT[0:72, hh, :], bdst
                    eng = alt()
                    if eng == "act":
                        P.op("act", lambda e, o_ap=o_ap, pbf=pbf: e.copy(out=o_ap, in_=pbf[0:72, 0:TT]),
                             reads=[b_ps[bk]], writes=[ob])
                    else:
                        P.op("dve", lambda e, o_ap=o_ap, pbf=pbf: e.tensor_copy(out=o_ap, in_=pbf[0:72, 0:TT]),
                             reads=[b_ps[bk]], writes=[ob])
                if which == "k":
                    return P.op("sp", lambda e, ksi=ksi: e.dma_start(out=kc[L][T, sq_i, :, :], in_=KS[ksi][0:72, :]),
                                reads=[bdst], lane=f"ks{ksi}")
                return None

            for s_ in range(8):
                qk_slab("q", s_)
            kst_tok = [qk_slab("k", s_) for s_ in range(8)]
            vst_tok = []
            for s_ in range(8):
                wv, wbuf = W.get(win, 0, 16, 2 * D + s_ * SLABW, SLABW, cin)
                vsi = rr["vs"]
                rr["vs"] ^= 1
                vv = VS[vsi][:].rearrange("p (a b c) -> p a b c", a=4, b=2)
                for tb in range(4):
                    bk = nbank()
                    fns = [lambda e, k=k, tb=tb, bk=bk, wv=wv: e.matmul(
                        psum[bk][:, 0:SLABW], lhsT=hn[:, k, tb * 128:(tb + 1) * 128], rhs=wv[:, k, :],
                        start=(k == 0), stop=(k == NCH - 1)) for k in range(NCH)]
                    P.pe_group(fns, reads=[wbuf] + b_hn, writes=[b_ps[bk]])
                    pv = psum[bk][:, 0:SLABW].rearrange("p (a b d) -> p a b d", a=2, b=2)
                    for par in range(2):
                        eng = alt()
                        o_ap = vv[:, tb, :, 128 * par:128 * par + 64]
                        i_ap = pv[:, :, par, :]
                        if eng == "act":
                            P.op("act", lambda e, o_ap=o_ap, i_ap=i_ap: e.copy(out=o_ap, in_=i_ap),
                                 reads=[b_ps[bk]], writes=[b_VS[vsi]])
                        else:
                            P.op("dve", lambda e, o_ap=o_ap, i_ap=i_ap: e.tensor_copy(out=o_ap, in_=i_ap),
                                 reads=[b_ps[bk]], writes=[b_VS[vsi]])
                vst_tok.append(P.op("sp", lambda e, vsi=vsi, s_=s_: e.dma_start(out=vc[L][T, s_, :, :], in_=VS[vsi][:]),
                                    reads=[b_VS[vsi]], lane=f"vs{vsi}"))
            for s_ in range(8):
                wv, wbuf = W.get(win, 0, 16, 3 * D + s_ * SLABW, SLABW, cin)
                for jj in range(2):
                    bk = nbank()
                    fns = [lambda e, k=k, jj=jj, bk=bk, wv=wv: e.matmul(
                        psum[bk][:], lhsT=wv[:, k, jj * 128:(jj + 1) * 128], rhs=hn[:, k, :],
                        start=(k == 0), stop=(k == NCH - 1)) for k in range(NCH)]
                    P.pe_group(fns, reads=[wbuf] + b_hn, writes=[b_ps[bk]])
                    P.op("act", lambda e, bk=bk, ch=2 * s_ + jj: e.activation(out=OG[:, ch, :], in_=psum[bk][:],
                                                                             func=AF.Sigmoid),
                         reads=[b_ps[bk]], writes=[b_OG[2 * s_ + jj]])

            kreqs = []
            vreqs = []
            for hg in range(8):
                for i in range(T + 1):
                    kd = [kst_tok[hg]] if i == T else []
                    vd = [vst_tok[hg]] if i == T else []
                    kreqs.append((lambda t_: t_[0:72, :], kc[L][i, hg, :, :], kd))
                    vreqs.append((lambda t_: t_[:], vc[L][i, hg, :, :], vd))
            if not P.dry:
                KSt.begin(kreqs)
                VSt.begin(vreqs)
                for r_ in range(2):
                    KSt.issue(r_)
                    VSt.issue(r_)
            OB = [0, 1, 2, 3]
            SB = [4, 5, 6]
            BC = 7
            for hg in range(8):
                steps = []
                for i in range(T + 1):
                    for kb in range(4):
                        for hh in range(4):
                            steps.append((i, kb, hh))
                kv = {}
                if not P.dry:
                    for i in range(T + 1):
                        kt, kbuf = KSt.slot(hg * (T + 1) + i)
                        vt, vbuf = VSt.slot(hg * (T + 1) + i)
                        kv[i] = (kt[:].rearrange("p (h t) -> p h t", t=TT), kbuf,
                                 vt[:].rearrange("p (a b c) -> p a b c", a=4, b=2), vbuf)
                else:
                    continue

                def emit_S(n):
                    i, kb, hh = steps[n]
                    h = 4 * hg + hh
                    kt, kbuf, _, _ = kv[i]
                    qlo = kb * 128 if i == T else 0
                    nco = TT - qlo
                    sbk = SB[n % 3]
                    fns = [lambda e: e.matmul(psum[sbk][:, 0:nco], lhsT=kt[0:70, hh, kb * 128:(kb + 1) * 128],
                                              rhs=QT[0:70, h, qlo:TT], start=True, stop=(i != T))]
                    if i == T:
                        fns.append(lambda e: e.matmul(psum[sbk][:, 0:128], lhsT=ident_bf[:], rhs=mask_bf[:],
                                                      start=False, stop=True))
                    P.pe_group(fns, reads=[kbuf, bQT, b_c2], writes=[b_ps[sbk]])

                def emit_E(n):
                    i, kb, hh = steps[n]
                    h = 4 * hg + hh
                    qlo = kb * 128 if i == T else 0
                    nco = TT - qlo
                    sbk = SB[n % 3]
                    pi = n % 3
                    P.op("act", lambda e: e.activation(out=PT[pi][:, 0:nco], in_=psum[sbk][:, 0:nco], func=AF.Exp,
                                                       bias=DTAB[:, i, h:h + 1], scale=1.0),
                         reads=[b_ps[sbk], b_DT], writes=[b_PT[pi]])

                def emit_PV(n):
                    i, kb, hh = steps[n]
                    _, _, vt, vbuf = kv[i]
                    qlo = kb * 128 if i == T else 0
                    nco = TT - qlo
                    pi = n % 3
                    obk = OB[hh]
                    pr = hh // 2
                    first = (i == 0 and kb == 0)
                    last = (i == T and kb == 3)
                    if hh % 2 == 0:
                        fn = lambda e: e.matmul(psum[obk][0:65, qlo:TT], lhsT=vt[:, kb, pr, 0:65],
                                                rhs=PT[pi][:, 0:nco], start=first, stop=last)
                    else:
                        fn = lambda e: e.matmul(psum[obk][:, qlo:TT], lhsT=vt[:, kb, pr, 64:192],
                                                rhs=PT[pi][:, 0:nco], start=first, stop=last)
                    P.pe_group([fn], reads=[vbuf, b_PT[pi]], writes=[b_ps[obk]])

                ns = len(steps)
                emit_S(0)
                if ns > 1:
                    emit_S(1)
                for n in range(ns):
                    emit_E(n)
                    if n + 2 < ns:
                        emit_S(n + 2)
                    emit_PV(n)
                    if n == ns - 1 or steps[n + 1][0] != steps[n][0]:
                        r_ = hg * (T + 1) + steps[n][0]
                        KSt.issue(r_ + 2)
                        VSt.issue(r_ + 2)
                for pr in range(2):
                    obe, obo = OB[2 * pr], OB[2 * pr + 1]
                    ch = 2 * hg + pr
                    P.op("act", lambda e, obe=obe: e.copy(out=RR[64:65, :], in_=psum[obe][64:65, :]),
                         reads=[b_ps[obe]], writes=[b_RR])
                    P.op("act", lambda e, obo=obo: e.copy(out=RR[0:1, :], in_=psum[obo][0:1, :]),
                         reads=[b_ps[obo]], writes=[b_RR])
                    P.pe_group([lambda e: e.matmul(psum[BC][:], lhsT=cst_sb[:, C_SEL:C_SEL + 128], rhs=RR[:],
                                                   start=True, stop=True)],
                               reads=[b_RR, b_cst], writes=[b_ps[BC]])
                    P.op("dve", lambda e: e.reciprocal(out=RC[:], in_=psum[BC][:]), reads=[b_ps[BC]], writes=[b_RC])
                    P.op("dve", lambda e, ch=ch: e.tensor_tensor(out=RC[:], in0=RC[:], in1=OG[:, ch, :], op=ALU.mult),
                         reads=[b_RC, b_OG[ch]], writes=[b_RC])
                    P.op("dve", lambda e, ch=ch, obe=obe: e.tensor_tensor(
                        out=OG[0:64, ch, :], in0=psum[obe][0:64, :], in1=RC[0:64, :], op=ALU.mult),
                        reads=[b_ps[obe], b_RC], writes=[b_OG[ch]])
                    P.op("dve", lambda e, ch=ch, obo=obo: e.tensor_tensor(
                        out=OG[64:128, ch, :], in0=psum[obo][64:128, :], in1=RC[64:128, :], op=ALU.mult),
                        reads=[b_ps[obo], b_RC], writes=[b_OG[ch]])
            rr["bank"] = 0
            for s_ in range(8):
                wv, wbuf = W.get(wout, 0, 16, s_ * SLABW, SLABW, cout)
                for jj in range(2):
                    bk = nbank()
                    fns = [lambda e, k=k, jj=jj, bk=bk, wv=wv: e.matmul(
                        psum[bk][:], lhsT=wv[:, k, jj * 128:(jj + 1) * 128], rhs=OG[:, k, :],
                        start=(k == 0), stop=(k == NCH - 1)) for k in range(NCH)]
                    P.pe_group(fns, reads=[wbuf] + b_OG, writes=[b_ps[bk]])
                    resid_add(2 * s_ + jj, bk)

        def store_out(T):
            if do_final:
                norm_stats()
                gcol = 128
            (b_os,) = reg1.switch(["ostage"])
            os_ = R1[:, 0:16384].bitcast(F32).rearrange("p (t d) -> p t d", t=4)
            fin = R1[:, 16384:16384 + 1024].bitcast(F32)
            (b_fin,) = [Buf("fin")]
            b_fin.r = list(b_os.r)
            for c in range(NCH):
                if do_final:
                    P.op("dve", lambda e, c=c: e.scalar_tensor_tensor(
                        out=fin, in0=hT[:, c, :], scalar=G(gcol + c), in1=rstd[:], op0=ALU.mult, op1=ALU.mult),
                        reads=[b_hT[c], b_rstd, b_cst], writes=[b_fin])
                    src_ap, src_b = fin, b_fin
                else:
                    src_ap, src_b = hT[:, c, :], b_hT[c]
                bk = nbank()
                fns = [lambda e, tb=tb, bk=bk, src_ap=src_ap: e.transpose(
                    out=psum[bk][:, tb * 128:(tb + 1) * 128], in_=src_ap[:, tb * 128:(tb + 1) * 128],
                    identity=cst_sb[:, C_ID:C_ID + 128]) for tb in range(4)]
                P.pe_group(fns, reads=[src_b, b_cst], writes=[b_ps[bk]])
                eng = alt()
                o_ap = os_[:, :, c * 128:(c + 1) * 128]
                i_ap = psum[bk][:].rearrange("p (t d) -> p t d", t=4)
                if eng == "act":
                    P.op("act", lambda e, o_ap=o_ap, i_ap=i_ap: e.copy(out=o_ap, in_=i_ap),
                         reads=[b_ps[bk]], writes=[b_os])
                else:
                    P.op("dve", lambda e, o_ap=o_ap, i_ap=i_ap: e.tensor_copy(out=o_ap, in_=i_ap),
                         reads=[b_ps[bk]], writes=[b_os])
            dst = out[T * TT:(T + 1) * TT, :].rearrange("(t p) d -> p t d", p=128)
            return P.op("sp", lambda e: e.dma_start(out=dst, in_=os_), reads=[b_os], lane="o")

        def walk():
            last = None
            for T in range(NT):
                load_x(T)
                for L in layers:
                    if L % 2 == 0:
                        pool_layer(L, T)
                    else:
                        fox_layer(L, T)
                    ffn_layer(L)
                last = store_out(T)
            return last

        emit_init()
        P.dry = True
        walk()
        P.dry = False
        W.pos = 0
        for k_ in rr:
            rr[k_] = 0
        reg1.live = []
        last = walk()
        P.lists["sp"].append(([(last.sem, last.val)], None, None, 0))

        with nc.Block() as block:
            def runner(eng):
                def f(e):
                    for waits, fn, tok, inc in P.lists[eng]:
                        for s_, v_ in waits:
                            e.wait_ge(s_, v_)
                        if fn is None:
                            continue
                        ins = fn(e)
                        if tok is not None:
                            ins.then_inc(tok.sem, inc)
                return f

            block.tensor(runner("pe"))
            block.scalar(runner("act"))
            block.vector(runner("dve"))
            block.gpsimd(runner("pool"))
            block.sync(runner("sp"))
    return nc


def make_consts(inputs):
    c = np.zeros((128, CW), np.float32)
    c[:, C_ID:C_ID + 128] = np.eye(128, dtype=np.float32)
    r = np.arange(128)
    c[:, C_NTRI:C_NTRI + 128] = -(r[:, None] <= r[None, :]).astype(np.float32)
    c[:, C_MASK:C_MASK + 128] = np.where(r[:, None] > r[None, :], MASKV, 0.0).astype(np.float32)
    c[64, C_SEL:C_SEL + 64] = 1.0
    c[0, C_SEL + 64:C_SEL + 128] = 1.0

    def fm(v):
        return np.asarray(v, np.float32).reshape(16, 128).T

    for L in range(4):
        c[:, C_G + 16 * L:C_G + 16 * (L + 1)] = fm(inputs["attn_norm_g"][L])
        c[:, C_G + 64 + 16 * L:C_G + 64 + 16 * (L + 1)] = fm(inputs["ffn_norm_g"][L])
    c[:, C_G + 128:C_G + 144] = fm(inputs["final_norm_g"])
    for j in range(2):
        c[:, C_G + 144 + 16 * j:C_G + 160 + 16 * j] = fm(inputs["pool_scale"][j])
        c[:, C_BF + 32 * j:C_BF + 32 * (j + 1)] = np.asarray(inputs["fox_b_f"][j], np.float32)[None, :]
        c[:, C_QG + 64 * j:C_QG + 64 * (j + 1)] = np.asarray(inputs["fox_q_norm_g"][j], np.float32)[None, :]
        c[:, C_KG + 64 * j:C_KG + 64 * (j + 1)] = np.asarray(inputs["fox_k_norm_g"][j], np.float32)[None, :]
    return c


def weight_map(inputs, layers):
    m = {}
    for L in layers:
        j = L // 2
        if L % 2 == 0:
            m[f"pw{j}"] = np.ascontiguousarray(np.asarray(inputs["pool_w"][j], np.float32).reshape(2048, 512))
        else:
            m[f"win{j}"] = np.ascontiguousarray(np.asarray(inputs["fox_w_in"][j], np.float32))
            m[f"wout{j}"] = np.ascontiguousarray(np.asarray(inputs["fox_w_out"][j], np.float32))
        m[f"wgu{L}"] = np.ascontiguousarray(np.asarray(inputs["ffn_w_gate_up"][L], np.float32))
        m[f"wdn{L}"] = np.ascontiguousarray(np.asarray(inputs["ffn_w_down"][L], np.float32))
    return m


def run_layers(xs, inputs, layers, do_final, NT):
    nc = build(NT, layers, do_final)
    cst = make_consts(inputs)
    wm = weight_map(inputs, layers)
    in_maps = []
    for xb in xs:
        d = {"x": np.ascontiguousarray(xb, dtype=np.float32), "cst": cst}
        d.update(wm)
        in_maps.append(d)
    res = run_bass_kernel_spmd(nc, in_maps, core_ids=list(range(len(xs))))
    return [np.asarray(r["out"]) for r in res.results]


FUSED = True


def kernel(**inputs):
    x = np.asarray(inputs["x"], np.float32)
    B, S, _ = x.shape
    NT = S // TT
    xs = [x[b] for b in range(B)]
    if FUSED:
        outs = run_layers(xs, inputs, [0, 1, 2, 3], True, NT)
    else:
        for L in range(4):
            xs = run_layers(xs, inputs, [L], L == 3, NT)
        outs = xs
    return np.stack(outs, axis=0).astype(np.float32)
```
